# Optimizing a Trainium2 kernel written in Bass

```python
import math
import jax, jax.numpy as jnp
from jax import lax
import numpy as np

D_MODEL = 1024
BATCH = 2
SEQ = 16384
DEPTH = 2
DEC_BATCH = 32
DEC_SEQ = 16
PAST_LEN = 4096

CHUNK = 64
MIX_W = D_MODEL
ATT_HEADS = 8
ATT_KV_HEADS = 2
ATT_GROUP = ATT_HEADS // ATT_KV_HEADS
ATT_HEAD_DIM = 64
ATT_Q_W = ATT_HEADS * ATT_HEAD_DIM
ATT_KV_W = ATT_KV_HEADS * ATT_HEAD_DIM
ATT_SCALE = ATT_HEAD_DIM ** -0.5
WINDOW = 128
BAND_CHUNKS = -(-WINDOW // CHUNK)
BAND = (BAND_CHUNKS + 1) * CHUNK
REL_BUCKETS = 32
REL_MAX_DIST = 128
RET_HEADS = 4
RET_KEY_DIM = 128
RET_VAL_DIM = 128
RET_QK_W = RET_HEADS * RET_KEY_DIM
RET_V_W = RET_HEADS * RET_VAL_DIM
ROPE_BASE = 10000.0
IN_SIZES = (ATT_Q_W, ATT_KV_W, ATT_KV_W, RET_QK_W, RET_QK_W, RET_V_W, RET_V_W)
IN_W = sum(IN_SIZES)
D_FF = 4 * D_MODEL
N_MOD = 6
EPS = 1e-6

kernel_name = 'hymba_swa_sink_retention_stream_step'


def rms_norm(x, g):
    xf = x.astype(jnp.float32)
    y = xf * lax.rsqrt(jnp.mean(xf * xf, axis=-1, keepdims=True) + EPS)
    return (y * g.astype(jnp.float32)).astype(x.dtype)


def modulate(x, g, shift, scale):
    return rms_norm(x, g) * (1.0 + scale[:, None, :]) + shift[:, None, :]


def ada_params(c, w, b):
    m = jnp.einsum('bd,de->be', jax.nn.silu(c), w) + b
    return jnp.split(m, N_MOD, axis=-1)


def rotary(x, pos):
    half = x.shape[-1] // 2
    inv = 1.0 / (ROPE_BASE ** (jnp.arange(half, dtype=jnp.float32) / half))
    ang = pos.astype(jnp.float32)[:, None] * inv[None, :]
    cos = jnp.cos(ang)[:, None, :]
    sin = jnp.sin(ang)[:, None, :]
    xf = x.astype(jnp.float32)
    x1, x2 = xf[..., :half], xf[..., half:]
    return jnp.concatenate([x1 * cos - x2 * sin, x2 * cos + x1 * sin], axis=-1).astype(x.dtype)


def rel_bucket(rel):
    nb = REL_BUCKETS // 2
    n = -rel
    ret = jnp.where(n < 0, nb, 0)
    n = jnp.abs(n)
    max_exact = nb // 2
    nf = jnp.maximum(n, 1).astype(jnp.float32)
    large = max_exact + (jnp.log(nf / max_exact) / math.log(REL_MAX_DIST / max_exact)
                         * (nb - max_exact)).astype(jnp.int32)
    large = jnp.minimum(large, nb - 1)
    return ret + jnp.where(n < max_exact, n, large)


def relative_bias(rel, table):
    b = table.astype(jnp.float32)[rel_bucket(rel)]
    q, j = rel.shape
    return jnp.transpose(b, (2, 0, 1)).reshape(ATT_KV_HEADS, ATT_GROUP, q, j)


def sink_softmax(s, sinks):
    sink = sinks.astype(jnp.float32).reshape(ATT_KV_HEADS, ATT_GROUP, 1, 1)
    m = jnp.maximum(jnp.max(s, axis=-1, keepdims=True), sink)
    e = jnp.exp(s - m)
    return e / (jnp.sum(e, axis=-1, keepdims=True) + jnp.exp(sink - m))


def project(h, w_in, pos):
    bsz, seq = h.shape[:2]
    u = jnp.einsum('bsd,de->bse', h, w_in)
    cuts = [sum(IN_SIZES[:i + 1]) for i in range(len(IN_SIZES) - 1)]
    qa, ka, va, qr, kr, vr, gr = jnp.split(u, cuts, axis=-1)
    qa = qa.reshape(bsz, seq, ATT_HEADS, ATT_HEAD_DIM)
    ka = ka.reshape(bsz, seq, ATT_KV_HEADS, ATT_HEAD_DIM)
    va = va.reshape(bsz, seq, ATT_KV_HEADS, ATT_HEAD_DIM)
    qr = rotary(qr.reshape(bsz, seq, RET_HEADS, RET_KEY_DIM), pos)
    kr = rotary(kr.reshape(bsz, seq, RET_HEADS, RET_KEY_DIM), pos) * (RET_KEY_DIM ** -0.5)
    vr = vr.reshape(bsz, seq, RET_HEADS, RET_VAL_DIM)
    return qa, ka, va, qr, kr, vr, gr


def window_attention_prompt(q, k, v, sinks, rel_table):
    bsz, seq = q.shape[:2]
    nc = seq // CHUNK
    qc = q.reshape(bsz, nc, CHUNK, ATT_KV_HEADS, ATT_GROUP, ATT_HEAD_DIM)
    pad = ((0, 0), (BAND_CHUNKS * CHUNK, 0), (0, 0), (0, 0))
    kp = jnp.pad(k, pad).reshape(bsz, nc + BAND_CHUNKS, CHUNK, ATT_KV_HEADS, ATT_HEAD_DIM)
    vp = jnp.pad(v, pad).reshape(bsz, nc + BAND_CHUNKS, CHUNK, ATT_KV_HEADS, ATT_HEAD_DIM)
    kb = jnp.concatenate([kp[:, j:j + nc] for j in range(BAND_CHUNKS + 1)], axis=2)
    vb = jnp.concatenate([vp[:, j:j + nc] for j in range(BAND_CHUNKS + 1)], axis=2)
    s = jnp.einsum('bcikgd,bcjkd->bckgij', qc, kb).astype(jnp.float32) * ATT_SCALE
    qi = jnp.arange(CHUNK)
    kj = jnp.arange(BAND) - BAND_CHUNKS * CHUNK
    s = s + relative_bias(kj[None, :] - qi[:, None], rel_table)
    key_chunk = jnp.arange(nc)[:, None] + (jnp.arange(BAND) // CHUNK - BAND_CHUNKS)[None, :]
    s = jnp.where((key_chunk >= 0)[None, :, None, None, None, :], s, -jnp.inf)
    p = sink_softmax(s, sinks)
    o = jnp.einsum('bckgij,bcjkd->bcikgd', p.astype(vb.dtype), vb)
    return o.reshape(bsz, seq, ATT_Q_W)


def window_attention_sample(q, k_all, v_all, n_cache, sinks, rel_table):
    dbsz, n = q.shape[:2]
    qg = q.reshape(dbsz, n, ATT_KV_HEADS, ATT_GROUP, ATT_HEAD_DIM)
    s = jnp.einsum('bikgd,bjkd->bkgij', qg, k_all).astype(jnp.float32) * ATT_SCALE
    qpos = PAST_LEN + jnp.arange(n)
    kpos = PAST_LEN - n_cache + jnp.arange(n_cache + n)
    s = s + relative_bias(kpos[None, :] - qpos[:, None], rel_table)
    p = sink_softmax(s, sinks)
    o = jnp.einsum('bkgij,bjkd->bikgd', p.astype(v_all.dtype), v_all)
    return o.reshape(dbsz, n, ATT_Q_W)


def ret_log_decay():
    return jnp.log(1.0 - 2.0 ** (-5.0 - jnp.arange(RET_HEADS, dtype=jnp.float32)))


def causal_decay(n, log_g):
    i = jnp.arange(n)
    diff = (i[:, None] - i[None, :]).astype(jnp.float32)
    mask = diff >= 0
    return jnp.where(mask[None], jnp.exp(jnp.where(mask, diff, 0.0)[None] * log_g[:, None, None]), 0.0)


def retention_prompt(q, k, v):
    bsz, seq = q.shape[:2]
    nc = seq // CHUNK
    log_g = ret_log_decay()
    qc = q.astype(jnp.float32).reshape(bsz, nc, CHUNK, RET_HEADS, RET_KEY_DIM)
    kc = k.astype(jnp.float32).reshape(bsz, nc, CHUNK, RET_HEADS, RET_KEY_DIM)
    vc = v.astype(jnp.float32).reshape(bsz, nc, CHUNK, RET_HEADS, RET_VAL_DIM)
    pos = jnp.arange(CHUNK, dtype=jnp.float32)
    scores = jnp.einsum('bcihd,bcjhd->bchij', qc, kc) * causal_decay(CHUNK, log_g)
    intra = jnp.einsum('bchij,bcjhe->bcihe', scores, vc)
    kdec = jnp.exp((CHUNK - 1.0 - pos)[:, None] * log_g[None, :])
    delta = jnp.einsum('bcjhd,bcjhe->cbhde', kc * kdec[:, :, None], vc)
    chunk_decay = jnp.exp(CHUNK * log_g)[None, :, None, None]

    def step(s_prev, d):
        return chunk_decay * s_prev + d, s_prev

    s0 = jnp.zeros((bsz, RET_HEADS, RET_KEY_DIM, RET_VAL_DIM), jnp.float32)
    s_final, s_before = lax.scan(step, s0, delta)
    qdec = jnp.exp((pos + 1.0)[:, None] * log_g[None, :])
    inter = jnp.einsum('bcihd,cbhde->bcihe', qc * qdec[:, :, None], s_before)
    return (intra + inter).reshape(bsz, seq, RET_HEADS, RET_VAL_DIM), s_final


def retention_sample(q, k, v, s0):
    n = q.shape[1]
    log_g = ret_log_decay()
    qf, kf, vf = q.astype(jnp.float32), k.astype(jnp.float32), v.astype(jnp.float32)
    s0f = s0.astype(jnp.float32)
    pos = jnp.arange(n, dtype=jnp.float32)
    scores = jnp.einsum('bihd,bjhd->bhij', qf, kf) * causal_decay(n, log_g)
    intra = jnp.einsum('bhij,bjhe->bihe', scores, vf)
    qdec = jnp.exp((pos + 1.0)[:, None] * log_g[None, :])
    inter = jnp.einsum('bihd,bhde->bihe', qf * qdec[:, :, None], s0f)
    kdec = jnp.exp((n - 1.0 - pos)[:, None] * log_g[None, :])
    s_new = jnp.exp(n * log_g)[None, :, None, None] * s0f + jnp.einsum('bjhd,bjhe->bhde', kf * kdec[:, :, None], vf)
    return intra + inter, s_new


def merge_heads(att_o, ret_o, gr, w_out):
    bsz, seq = att_o.shape[:2]
    r = ret_o * lax.rsqrt(jnp.mean(ret_o * ret_o, axis=-1, keepdims=True) + EPS)
    r = r.reshape(bsz, seq, RET_V_W).astype(gr.dtype) * jax.nn.silu(gr)
    mixed = jnp.concatenate([att_o.astype(gr.dtype), r], axis=-1)
    return jnp.einsum('bse,ed->bsd', mixed, w_out)


def squared_relu_mlp(h, w_up, w_down):
    u = jnp.einsum('bsd,df->bsf', h, w_up)
    return jnp.einsum('bsf,fd->bsd', jnp.square(jax.nn.relu(u)), w_down)


def setup_inputs(seed: int = 0) -> dict:
    key = jax.random.key(seed)
    ks = jax.random.split(key, 20)
    f32 = jnp.float32
    keep = min(WINDOW, PAST_LEN)

    def nrm(k, shape, scale):
        return jax.random.normal(k, shape, f32) * scale

    return {
        'x_prompt': nrm(ks[0], (BATCH, SEQ, D_MODEL), 1.0),
        'x_sample': nrm(ks[1], (DEC_BATCH, DEC_SEQ, D_MODEL), 1.0),
        'c_prompt': nrm(ks[2], (BATCH, D_MODEL), 1.0),
        'c_sample': nrm(ks[3], (DEC_BATCH, D_MODEL), 1.0),
        'cache_win_k': nrm(ks[4], (DEPTH, DEC_BATCH, keep, ATT_KV_HEADS, ATT_HEAD_DIM), 1.0),
        'cache_win_v': nrm(ks[5], (DEPTH, DEC_BATCH, keep, ATT_KV_HEADS, ATT_HEAD_DIM), 1.0),
        'state_ret': nrm(ks[6], (DEPTH, DEC_BATCH, RET_HEADS, RET_KEY_DIM, RET_VAL_DIM), 0.5),
        'g_mix': 1.0 + nrm(ks[7], (DEPTH, D_MODEL), 0.1),
        'g_mlp': 1.0 + nrm(ks[8], (DEPTH, D_MODEL), 0.1),
        'w_ada': nrm(ks[9], (DEPTH, D_MODEL, N_MOD * D_MODEL), D_MODEL ** -0.5),
        'b_ada': nrm(ks[10], (DEPTH, N_MOD * D_MODEL), 0.1),
        'w_in': nrm(ks[11], (DEPTH, D_MODEL, IN_W), D_MODEL ** -0.5),
        'w_out': nrm(ks[12], (DEPTH, MIX_W, D_MODEL), MIX_W ** -0.5),
        'att_sinks': nrm(ks[13], (DEPTH, ATT_HEADS), 1.0),
        'rel_bias': nrm(ks[14], (REL_BUCKETS, ATT_HEADS), 0.3),
        'w_up': nrm(ks[15], (DEPTH, D_MODEL, D_FF), D_MODEL ** -0.5),
        'w_down': nrm(ks[16], (DEPTH, D_FF, D_MODEL), D_FF ** -0.5),
        'g_final': 1.0 + nrm(ks[17], (D_MODEL,), 0.1),
    }


def reference(x_prompt, x_sample, c_prompt, c_sample, cache_win_k, cache_win_v, state_ret,
              g_mix, g_mlp, w_ada, b_ada, w_in, w_out, att_sinks, rel_bias, w_up, w_down, g_final):
    seq = x_prompt.shape[1]
    dec_seq = x_sample.shape[1]
    n_cache = cache_win_k.shape[2]
    keep_p = min(WINDOW, seq)
    pos_p = jnp.arange(seq)
    pos_s = PAST_LEN + jnp.arange(dec_seq)
    xp, xs = x_prompt, x_sample
    pk, pv, pr, sk, sv, sr = [], [], [], [], [], []
    for l in range(DEPTH):
        sh1p, sc1p, ga1p, sh2p, sc2p, ga2p = ada_params(c_prompt, w_ada[l], b_ada[l])
        sh1s, sc1s, ga1s, sh2s, sc2s, ga2s = ada_params(c_sample, w_ada[l], b_ada[l])

        h = modulate(xp, g_mix[l], sh1p, sc1p)
        qa, ka, va, qr, kr, vr, gr = project(h, w_in[l], pos_p)
        ao = window_attention_prompt(qa, ka, va, att_sinks[l], rel_bias)
        ro, st = retention_prompt(qr, kr, vr)
        xp = xp + ga1p[:, None, :] * merge_heads(ao, ro, gr, w_out[l])
        xp = xp + ga2p[:, None, :] * squared_relu_mlp(modulate(xp, g_mlp[l], sh2p, sc2p), w_up[l], w_down[l])
        pk.append(ka[:, seq - keep_p:])
        pv.append(va[:, seq - keep_p:])
        pr.append(st)

        h = modulate(xs, g_mix[l], sh1s, sc1s)
        qa, ka, va, qr, kr, vr, gr = project(h, w_in[l], pos_s)
        k_all = jnp.concatenate([cache_win_k[l].astype(ka.dtype), ka], axis=1)
        v_all = jnp.concatenate([cache_win_v[l].astype(va.dtype), va], axis=1)
        ao = window_attention_sample(qa, k_all, v_all, n_cache, att_sinks[l], rel_bias)
        ro, st = retention_sample(qr, kr, vr, state_ret[l])
        xs = xs + ga1s[:, None, :] * merge_heads(ao, ro, gr, w_out[l])
        xs = xs + ga2s[:, None, :] * squared_relu_mlp(modulate(xs, g_mlp[l], sh2s, sc2s), w_up[l], w_down[l])
        total = n_cache + dec_seq
        sk.append(k_all[:, total - n_cache:])
        sv.append(v_all[:, total - n_cache:])
        sr.append(st)

    y_prompt = rms_norm(xp, g_final)
    y_sample = rms_norm(xs, g_final)
    return (y_prompt, y_sample, jnp.stack(pk), jnp.stack(pv), jnp.stack(pr), jnp.stack(sk), jnp.stack(sv), jnp.stack(sr))
```

```python
import math
from contextlib import ExitStack
import numpy as np
import concourse.bass as bass
import concourse.mybir as mybir
from concourse.bass_utils import run_bass_kernel_spmd

F32 = mybir.dt.float32
BF16 = mybir.dt.bfloat16
AF = mybir.ActivationFunctionType
ALU = mybir.AluOpType

D = 1024
SEQ = 16384
NL = 2
DFF = 4096
NSB = 4
NS = 64
EPS = 1e-6
TM = 512
TF = 256
WIN_COLS = 2944
SAME_ENG_SYNC = True


class Res:
    __slots__ = ("name", "w", "r", "dsem", "dcnt")

    def __init__(self, name):
        self.name = name
        self.w = None
        self.r = {}
        self.dsem = None
        self.dcnt = 0


class Eng:
    def __init__(self, name, eng, sem, self_ok):
        self.name = name
        self.eng = eng
        self.sem = sem
        self.cnt = 0
        self.known = {}
        self.self_ok = self_ok


class Kern:
    def __init__(self, nc, es):
        self.nc = nc
        self.es = es
        self.nsem = 0
        mk = self.newsem
        self.pe = Eng("pe", nc.tensor, mk("pe"), True)
        self.act = Eng("act", nc.scalar, mk("act"), not SAME_ENG_SYNC)
        self.dve = Eng("dve", nc.vector, mk("dve"), not SAME_ENG_SYNC)
        self.pool = Eng("pool", nc.gpsimd, mk("pool"), not SAME_ENG_SYNC)
        self.sp = Eng("sp", nc.sync, mk("sp"), True)
        self.engs = [self.pe, self.act, self.dve, self.pool, self.sp]
        self.allres = []
        self.dsems = []

    def newsem(self, name):
        self.nsem += 1
        return self.es.enter_context(self.nc.semaphore(f"s_{name}_{self.nsem}"))

    def res(self, name):
        r = Res(name)
        self.allres.append(r)
        return r

    def sbuf(self, name, shape, dt):
        return self.es.enter_context(self.nc.sbuf_tensor(name, list(shape), dt))

    def _sync(self, E, reads, writes):
        need = {}

        def req(t):
            if t is None:
                return
            k = id(t[0])
            if k not in need or need[k][1] < t[1]:
                need[k] = t

        for r in reads:
            req(r.w)
            if r.name.startswith("ps"):
                for t in r.r.values():
                    if t[0] is not E.sem:
                        req(t)
        for w in writes:
            req(w.w)
            for t in w.r.values():
                req(t)
        for k, (sem, val) in need.items():
            if sem is E.sem and E.self_ok:
                continue
            if E.known.get(k, 0) >= val:
                continue
            E.eng.wait_ge(sem, val)
            E.known[k] = val

    @staticmethod
    def _post(t, reads, writes):
        k = id(t[0])
        for r in reads:
            if k not in r.r or r.r[k][1] < t[1]:
                r.r[k] = t
        for w in writes:
            w.w = t
            w.r = {}

    def op(self, E, fn, reads=(), writes=(), signal=True):
        self._sync(E, reads, writes)
        ins = fn()
        if signal:
            ins.then_inc(E.sem, 1)
            E.cnt += 1
            t = (E.sem, E.cnt)
        else:
            t = (E.sem, E.cnt + 1)
        self._post(t, reads, writes)

    def dma(self, Q, out, in_, reads, writes, semres):
        self._sync(Q, reads, writes)
        if semres.dsem is None:
            semres.dsem = {}
            semres.dcnt = {}
        kind = "sw" if Q is self.pool else "hw"
        if kind not in semres.dsem:
            semres.dsem[kind] = self.newsem("d" + kind + semres.name)
            semres.dcnt[kind] = 0
            self.dsems.append((semres, kind))
        semres.dcnt[kind] += 1
        Q.eng.dma_start(out=out, in_=in_).then_inc(semres.dsem[kind], 16)
        t = (semres.dsem[kind], 16 * semres.dcnt[kind])
        self._post(t, reads, writes)

    def barrier(self):
        tickets = {}
        for E in self.engs:
            if E.cnt > 0:
                tickets[id(E.sem)] = (E.sem, E.cnt)
        for r, kind in self.dsems:
            tickets[id(r.dsem[kind])] = (r.dsem[kind], 16 * r.dcnt[kind])
        for E in self.engs:
            for k, (sem, val) in tickets.items():
                if sem is E.sem:
                    continue
                if E.known.get(k, 0) >= val:
                    continue
                E.eng.wait_ge(sem, val)
                E.known[k] = val

    def mm(self, out, lhsT, rhs, start, stop, reads, writes, signal=None):
        nc = self.nc
        if signal is None:
            signal = stop
        self.op(self.pe, lambda: nc.tensor.matmul(out, lhsT=lhsT, rhs=rhs, start=start, stop=stop),
                reads, writes, signal)

    def tr(self, out, in_, ident, reads, writes, signal=True):
        nc = self.nc
        self.op(self.pe, lambda: nc.tensor.transpose(out, in_, ident), reads, writes, signal)

    def actf(self, out, in_, func, reads, writes, bias=None, scale=None, accum_out=None):
        nc = self.nc
        kw = {}
        if bias is not None:
            kw["bias"] = bias
        if scale is not None:
            kw["scale"] = scale
        if accum_out is not None:
            kw["accum_out"] = accum_out
        self.op(self.act, lambda: nc.scalar.activation(out=out, in_=in_, func=func, **kw), reads, writes)

    def tt(self, out, in0, in1, op, reads, writes, E=None):
        E = E or self.dve
        self.op(E, lambda: E.eng.tensor_tensor(out=out, in0=in0, in1=in1, op=op), reads, writes)

    def ts(self, out, in0, s1, s2, op0, op1, reads, writes, E=None):
        E = E or self.dve
        if op1 is None:
            self.op(E, lambda: E.eng.tensor_scalar(out=out, in0=in0, scalar1=s1, scalar2=None, op0=op0),
                    reads, writes)
        else:
            self.op(E, lambda: E.eng.tensor_scalar(out=out, in0=in0, scalar1=s1, scalar2=s2, op0=op0, op1=op1),
                    reads, writes)

    def stt(self, out, in0, scalar, in1, op0, op1, reads, writes):
        nc = self.nc
        self.op(self.dve, lambda: nc.vector.scalar_tensor_tensor(out=out, in0=in0, scalar=scalar, in1=in1,
                                                                 op0=op0, op1=op1), reads, writes)

    def cp(self, out, in_, reads, writes, E=None):
        E = E or self.dve
        self.op(E, lambda: E.eng.tensor_copy(out=out, in_=in_), reads, writes)

    def memset(self, ap, val, writes, E=None):
        E = E or self.dve
        self.op(E, lambda: E.eng.memset(ap, val), (), writes)


class _Stop(Exception):
    pass


def build_program(SEQ=SEQ, stop=None):
    nc = bass.Bass("TRN2", target_bir_lowering=False)
    es = ExitStack()
    K = Kern(nc, es)

    def din(name, shape, dt=F32):
        return nc.dram_tensor(name, list(shape), dt, kind="ExternalInput")

    def dout(name, shape, dt=F32):
        return nc.dram_tensor(name, list(shape), dt, kind="ExternalOutput")

    def dscr(name, shape, dt=F32):
        return nc.dram_tensor(name, list(shape), dt, kind="Internal")

    xp = din("xp", [SEQ, D]).ap()
    xs_in = din("xs", [NS, D]).ap()
    cT_in = din("cT", [128, 8, 5]).ap()
    ck_in = din("cache_k", [NL, NSB, 128, 128]).ap()
    cv_in = din("cache_v", [NL, NSB, 128, 128]).ap()
    st_in = din("state", [NL, NSB, 128, 4, 128]).ap()
    gmix5_in = din("gmix5", [NL, 128, 8, 5]).ap()
    gmlp5_in = din("gmlp5", [NL, 128, 8, 5]).ap()
    gfin_in = din("gfin_bc", [128, D]).ap()
    wada_in = din("w_ada", [NL, D, 6 * D]).ap()
    bada5_in = din("bada5", [NL, 128, 48, 5]).ap()
    badarow_in = din("bada_row", [NL, 5, 2, D]).ap()
    win_in = din("w_in", [NL, D, 2816]).ap()
    wout_in = din("w_out", [NL, D, D]).ap()
    wup_in = din("w_up", [NL, D, DFF]).ap()
    wdn_in = din("w_down", [NL, DFF, D]).ap()
    winp_in = din("w_in_p", [NL, D, 2816]).ap()
    woutp_in = din("w_out_p", [NL, D, D]).ap()
    wupp_in = din("w_up_p", [NL, D, DFF]).ap()
    wdnp_in = din("w_down_p", [NL, DFF, D]).ap()
    sink_in = din("sinks", [NL, 128, 4]).ap()
    relb_in = din("rel_bias", [32, 8]).ap()
    cos_in = din("cosT", [128, SEQ]).ap()
    sin_in = din("sinT", [128, SEQ]).ap()
    coss_in = din("cosTs", [128, NS]).ap()
    sins_in = din("sinTs", [128, NS]).ap()
    cst_in = din("cst128", [128, 12, 128]).ap()
    oh_in = din("onehot", [32, 384]).ap()
    valid_in = din("valid", [128, 8, 256]).ap()
    rc128_in = din("retc128", [128, 4, 512]).ap()
    rc16_in = din("retc16", [128, 4, 512]).ap()

    y_p = dout("y_p", [SEQ, D]).ap()
    y_s = dout("y_s", [NS, D]).ap()
    wk_p = dout("wk_p", [NL, 128, 128]).ap()
    wv_p = dout("wv_p", [NL, 128, 128]).ap()
    st_p = dout("st_p", [NL, 128, 4, 128]).ap()
    wk_s = dout("wk_s", [NL, NSB, 128, 128]).ap()
    wv_s = dout("wv_s", [NL, NSB, 128, 128]).ap()
    st_s = dout("st_s", [NL, NSB, 128, 4, 128]).ap()

    xsc = [dscr("xsc0", [SEQ, D]).ap(), dscr("xsc1", [SEQ, D]).ap()]
    xssc = [dscr("xssc0", [NS, D]).ap(), dscr("xssc1", [NS, D]).ap()]
    gasc = dscr("gasc", [NL, 2, 192, D]).ap()
    usc_t = dscr("usc", [8, 384])
    esc = dscr("esc", [128, 8 * 256 + 8 * 16 + 8 * 16], BF16).ap()

    arena = K.sbuf("arena", [128, 65536], BF16)
    aux = K.sbuf("aux", [128, 8192], BF16)
    xt = [K.sbuf(f"xt{i}", [128, D], F32) for i in range(4)]
    xt_r = [K.res(f"xt{i}") for i in range(4)]
    xsb = K.sbuf("xsb", [128, 4, D], BF16)
    xsb_r = [K.res(f"xsb{i}") for i in range(4)]
    hT = K.sbuf("hT", [128, 8, TM], BF16)
    hT_r = K.res("hT")
    gabc = K.sbuf("gabc", [128, D], F32)
    gabc_r = K.res("gabc")
    gas = K.sbuf("gas", [NS, D], F32)
    gas_r = K.res("gas")
    gfin = K.sbuf("gfin", [128, D], F32)
    gfin_r = K.res("gfin")
    cst = K.sbuf("cst", [128, 12, 128], F32)
    cst_r = K.res("cst")
    cstb = K.sbuf("cstb", [128, 6, 128], BF16)
    cstb_r = K.res("cstb")
    mfm = K.sbuf("mfm", [128, 48, 5], F32)
    mfm_r = K.res("mfm")
    AB = K.sbuf("AB", [128, 2, 8, 5], F32)
    AB_r = K.res("AB")
    small = K.sbuf("small", [128, 64], F32)
    small_r = [K.res(f"small{i}") for i in range(16)]
    S = K.sbuf("S", [128, 4, 128], F32)
    S_r = K.res("S")
    f32a = K.sbuf("f32a", [128, 512], F32)
    f32a_r = K.res("f32a")
    f32b = K.sbuf("f32b", [128, 512], F32)
    f32b_r = K.res("f32b")
    f32c = K.sbuf("f32c", [128, 512], F32)
    f32c_r = K.res("f32c")
    esink = K.sbuf("esink", [128, 4], F32)
    esink_r = K.res("esink")
    scT = K.sbuf("scT", [128, 8, 5], BF16)
    scT_r = K.res("scT")
    cT = K.sbuf("cTsb", [128, 8, 5], F32)
    cT_r = K.res("cT")

    psf = [es.enter_context(nc.psum_tensor(f"psf{i}", [128, 512], F32)) for i in range(6)]
    psf_r = [K.res(f"psf{i}") for i in range(6)]
    psb = [es.enter_context(nc.psum_tensor(f"psb{i}", [128, 1024], BF16)) for i in range(2)]
    psb_r = [K.res(f"psb{i}") for i in range(2)]
    rot = {"f": 0, "b": 0}
    held = set()

    def nps():
        while True:
            i = rot["f"] % 6
            rot["f"] += 1
            if i not in held:
                return psf[i], psf_r[i]

    def npb():
        i = rot["b"] % 2
        rot["b"] += 1
        return psb[i], psb_r[i]

    def carve(base, off, nelem, dt, shape=None):
        ap = base[:, off:off + nelem]
        if dt == F32:
            ap = ap.bitcast(F32)
        return ap

    w_in_sb = arena[:, 0:8 * WIN_COLS].rearrange("p (k n) -> p k n", k=8)
    w_out_sb = arena[:, 23552:23552 + 8192].rearrange("p (k n) -> p k n", k=8)
    wM_r = K.res("wM")
    w_up_sb = arena[:, 0:32768].rearrange("p (k n) -> p k n", k=8)
    w_dn_sb = arena[:, 32768:65536].rearrange("p (k n) -> p k n", k=32)
    wF_r = K.res("wF")
    wD_r = K.res("wD")

    ov = {"off": 31744}

    def ovl(nelem, name):
        o = ov["off"]
        ov["off"] += nelem
        assert ov["off"] <= 65536, name
        return arena[:, o:o + nelem], K.res(name)

    qaT_f, qaT_r = ovl(4 * TM, "qaT")
    qaT = qaT_f.rearrange("p (m t) -> p m t", m=4)
    kT_f, kT_r = ovl(2 * 5 * 128, "kT")
    kT = kT_f.rearrange("p (v t) -> p v t", v=2)
    Vp_f, Vp_r = ovl(5 * 2 * 2 * 128, "Vpad")
    Vp = Vp_f.rearrange("p (s v h e) -> p s v h e", s=5, v=2, h=2)
    e_b = []
    for i in range(2):
        f, r = ovl(1024, f"e{i}")
        e_b.append((f.rearrange("p (h q) -> p h q", h=8), r))
    P_b = []
    for i in range(4):
        f, r = ovl(1024, f"P{i}")
        P_b.append((f.rearrange("p (h q) -> p h q", h=8), r))
    oT_f, oT_r = ovl(4 * TM, "oT")
    oT = oT_f.rearrange("p (m t) -> p m t", m=4)
    qrT_f, qrT_r = ovl(4 * TM, "qrT")
    qrT = qrT_f.rearrange("p (m t) -> p m t", m=4)
    krT_f, krT_r = ovl(4 * TM, "krT")
    krT = krT_f.rearrange("p (m t) -> p m t", m=4)
    grT_f, grT_r = ovl(4 * TM, "grT")
    grT = grT_f.rearrange("p (m t) -> p m t", m=4)
    vr_f, vr_r = ovl(4 * 512, "vr")
    vr = vr_f.rearrange("p (t e) -> p t e", t=4)
    mx_f, mx_r = ovl(4 * TM, "mixret")
    mx = mx_f.rearrange("p (m t) -> p m t", m=4)
    rt_b = [ovl(TM, f"rt{i}") for i in range(2)]
    kt_b = []
    for i in range(2):
        f, r = ovl(512, f"ktil{i}")
        kt_b.append((f.rearrange("p (h d) -> p h d", h=4), r))
    sc_b = []
    for i in range(2):
        f, r = ovl(512, f"scT{i}")
        sc_b.append((f.rearrange("p (h d) -> p h d", h=4), r))
    qt_f, qt_r = ovl(512, "qtil")
    qtl = qt_f.rearrange("p (h d) -> p h d", h=4)
    sq_f, sq_r = ovl(512, "sq")
    Sbf_b = []
    for i in range(2):
        f, r = ovl(512, f"Sbf{i}")
        Sbf_b.append((f.rearrange("p (h d) -> p h d", h=4), r))
    cs_b = []
    for i in range(2):
        f, r = ovl(2 * TM, f"cs{i}")
        cs_b.append((f.rearrange("p (c t) -> p c t", c=2), r))
    E_f, E_r = ovl(8 * 256, "E")
    E = E_f.rearrange("p (h q) -> p h q", h=8)
    Es0_f, Es0_r = ovl(8 * 16, "Es0")
    Es0 = Es0_f.rearrange("p (h q) -> p h q", h=8)
    Es1_f, Es1_r = ovl(8 * 16, "Es1")
    Es1 = Es1_f.rearrange("p (h q) -> p h q", h=8)

    rc128 = aux[:, 0:4096].bitcast(F32).rearrange("p (c n) -> p c n", c=4)
    rc16 = aux[:, 4096:8192].bitcast(F32).rearrange("p (c n) -> p c n", c=4)
    rcc_r = K.res("retc")
    uT = aux.rearrange("p (f t) -> p f t", f=32)
    uT_r = K.res("uT")
    aux_r = K.res("aux")

    ident = cstb[:, 0, :]
    Rt = cstb[:, 1, :]
    ones_bf = cstb[:, 2, :]
    opad = [cstb[:, 3, :], cstb[:, 4, :]]
    ident_f = cst[:, 0, :]
    J_f = cst[:, 2, :]

    SP, PL, ACT, DVE, PE = K.sp, K.pool, K.act, K.dve, K.pe

    def chk0(tag):
        if stop == tag:
            raise _Stop()

    try:
        K.dma(SP, cst[:], cst_in, (), [cst_r], cst_r)
        K.dma(SP, gfin[:], gfin_in, (), [gfin_r], gfin_r)
        K.dma(SP, cT[:], cT_in, (), [cT_r], cT_r)
        for i, j in enumerate([0, 1, 3, 4, 5]):
            K.cp(cstb[:, i, :], cst[:, j, :], [cst_r], [cstb_r])
        K.actf(scT[:], cT[:], AF.Silu, [cT_r], [scT_r])

        chk0("s1")
        relb = f32a[0:32, 0:8]
        ohs = f32b[0:32, 0:384]
        K.dma(SP, relb, relb_in, (), [f32a_r], f32a_r)
        K.dma(SP, ohs, oh_in, (), [f32b_r], f32b_r)
        pu, pu_r = nps()
        K.mm(pu[0:8, 0:384], relb, ohs, True, True, [f32a_r, f32b_r], [pu_r])
        usb = f32c[0:8, 0:384]
        K.actf(usb, pu[0:8, 0:384], AF.Copy, [pu_r], [f32c_r])
        usc_r = K.res("usc")
        K.dma(SP, usc_t.ap(), usb, [f32c_r], [usc_r], f32c_r)
        chk0("s2")
        valid_sb = xt[0][:, 0:2048].bitcast(BF16) if False else None
        erev = xt[1].rearrange("p (h q) -> p h q", h=8)[:, :, 0:128]
        vmask = xt[2].rearrange("p (h q) -> p h q", h=8)
        for half in range(2):
            src = bass.AP(usc_t, half * 128, [[1, 128], [384, 8], [1, 128]])
            K.dma(SP, erev, src, [usc_r], [xt_r[1]], xt_r[1])
            K.dma(SP, vmask, valid_in[:, :, half * 128:(half + 1) * 128], (), [xt_r[2]], xt_r[2])
            for hb in range(2):
                pb, pb_r = nps()
                K.mm(pb[:, :], J_f, xt[1][:, hb * 512:(hb + 1) * 512], True, True, [cst_r, xt_r[1]], [pb_r])
                vm = vmask[:, hb * 4:(hb + 1) * 4, :]
                t_neg = f32a[:, :].rearrange("p (h q) -> p h q", h=4)
                t_val = f32b[:, :].rearrange("p (h q) -> p h q", h=4)
                K.ts(t_neg, vm, 30000.0, -30000.0, ALU.mult, ALU.add, [xt_r[2]], [f32a_r])
                K.tt(t_val, pb[:, :].rearrange("p (h q) -> p h q", h=4), vm, ALU.mult, [pb_r, xt_r[2]], [f32b_r])
                K.tt(E[:, hb * 4:(hb + 1) * 4, half * 128:(half + 1) * 128], t_val, t_neg, ALU.add,
                     [f32a_r, f32b_r], [E_r])
        chk0("s3")
        er0 = xt[3][:, 0:128].rearrange("p (h q) -> p h q", h=8)
        src = bass.AP(usc_t, 128, [[1, 128], [384, 8], [1, 16]])
        K.dma(SP, er0, src, [usc_r], [xt_r[3]], xt_r[3])
        pb, pb_r = nps()
        K.mm(pb[:, 0:128], J_f, xt[3][:, 0:128], True, True, [cst_r, xt_r[3]], [pb_r])
        K.cp(Es0[:, :, :], pb[:, 0:128].rearrange("p (h q) -> p h q", h=8), [pb_r], [Es0_r])
        er1 = xt[0][0:16, 0:128].rearrange("p (h q) -> p h q", h=8)
        src = bass.AP(usc_t, 112, [[1, 16], [384, 8], [1, 16]])
        K.dma(SP, er1, src, [usc_r], [xt_r[0]], xt_r[0])
        pb, pb_r = nps()
        K.mm(pb[0:16, 0:128], cst[0:16, 2, 112:128], xt[0][0:16, 0:128], True, True, [cst_r, xt_r[0]], [pb_r])
        K.cp(Es1[0:16, :, :], pb[0:16, 0:128].rearrange("p (h q) -> p h q", h=8), [pb_r], [Es1_r])
        chk0("s4")
        esc_r = K.res("esc")
        K.dma(SP, esc[:, 0:2048], E_f, [E_r], [esc_r], E_r)
        K.dma(SP, esc[:, 2048:2176], Es0_f, [Es0_r], [esc_r], Es0_r)
        K.dma(SP, esc[0:16, 2176:2304], Es1_f[0:16, :], [Es1_r], [esc_r], Es1_r)


    except _Stop:
        pass
    def load_weights_M(l, prompt=True, consts=True):
        wv = (winp_in if prompt else win_in)[l].rearrange("(k p) n -> p k n", p=128)
        segs = [(0, 0, 512)]
        for kv in range(2):
            for dup in range(2):
                segs.append((512 + kv * 128 + dup * 64, 512 + kv * 64, 64))
        segs += [(768, 768, 512), (1280, 1280, 512), (1792, 2304, 512),
                 (2304, 640, 128), (2432, 1792, 512)]
        for dst, srcc, n in segs:
            K.dma(PL, w_in_sb[:, :, dst:dst + n], wv[:, :, srcc:srcc + n], (), [wM_r], wM_r)
        K.dma(PL, w_out_sb, (woutp_in if prompt else wout_in)[l].rearrange("(k p) n -> p k n", p=128), (), [wM_r],
              wM_r)
        if not consts:
            return
        K.dma(SP, E_f, esc[:, 0:2048], [esc_r], [E_r], E_r)
        K.dma(SP, Es0_f, esc[:, 2048:2176], [esc_r], [Es0_r], Es0_r)
        K.dma(SP, Es1_f[0:16, :], esc[0:16, 2176:2304], [esc_r], [Es1_r], Es1_r)
        K.dma(SP, rc128, rc128_in, (), [rcc_r], rcc_r)
        K.dma(SP, rc16, rc16_in, (), [rcc_r], rcc_r)
        K.dma(SP, esink[:], sink_in[l], (), [esink_r], esink_r)
        K.actf(esink[:], esink[:], AF.Exp, [esink_r], [esink_r])
        K.memset(Vp_f, 0.0, [Vp_r])
        K.memset(S[:], 0.0, [S_r])
        K.memset(Sbf_b[1][0], 0.0, [Sbf_b[1][1]])

    def load_weights_F(l, prompt=True):
        wu = wupp_in if prompt else wup_in
        for kk in range(8):
            K.dma(PL, w_up_sb[:, kk, :], wu[l, kk * 128:(kk + 1) * 128, :], (), [wF_r], wF_r)
        wd = (wdnp_in if prompt else wdn_in)[l].rearrange("(k p) n -> p k n", p=128)
        for q in range(4):
            K.dma(PL, w_dn_sb[:, q * 8:(q + 1) * 8, :], wd[:, q * 8:(q + 1) * 8, :], (), [wD_r], wD_r)

    def ada(l):
        held.clear()
        pa, pa_r = nps()
        held.add(psf.index(pa))
        wa = [aux[:, 0:4096].rearrange("p (k n) -> p k n", k=8), aux[:, 4096:8192].rearrange("p (k n) -> p k n", k=8)]
        wa_r = [K.res("wa0"), K.res("wa1")]
        mrow = xt[3].rearrange("p (a n) -> p a n", a=1)
        mrow_t = [xt[2], xt[3]]
        mrow_r = [xt_r[2], xt_r[3]]
        brow = xt[1]
        K.dma(SP, brow[0:5, :], badarow_in[l, :, 0, :], (), [xt_r[1]], xt_r[1])
        brow2 = xt[0]
        K.dma(SP, brow2[0:5, :], badarow_in[l, :, 1, :], (), [xt_r[0]], xt_r[0])
        for j in range(12):
            w = wa[j % 2]
            wr = wa_r[j % 2]
            K.dma(PL, w, wada_in[l].rearrange("(k p) n -> p k n", p=128)[:, :, j * 512:(j + 1) * 512], (), [wr], wr)
            for cb in range(4):
                blk = j * 4 + cb
                for kk in range(8):
                    K.mm(pa[:, blk * 5:(blk + 1) * 5], w[:, kk, cb * 128:(cb + 1) * 128], scT[:, kk, :],
                         kk == 0, kk == 7, [wr, scT_r], [pa_r])
            if j in (4, 5, 10, 11):
                which = 0 if j < 6 else 1
                half = j % 2
                pr, pr_r = nps()
                for kk in range(8):
                    K.mm(pr[0:5, :], scT[:, kk, :], w[:, kk, :], kk == 0, kk == 7, [wr, scT_r], [pr_r])
                bsrc = (brow if which == 0 else brow2)
                bsr = (xt_r[1] if which == 0 else xt_r[0])
                K.tt(mrow_t[which][0:5, half * 512:(half + 1) * 512], pr[0:5, :],
                     bsrc[0:5, half * 512:(half + 1) * 512], ALU.add, [pr_r, bsr], [mrow_r[which]])
        b5 = f32a[:, 0:240].rearrange("p (a b) -> p a b", b=5)
        K.dma(SP, b5, bada5_in[l], (), [f32a_r], f32a_r)
        K.tt(mfm[:], pa[:, 0:240].rearrange("p (a b) -> p a b", b=5), b5, ALU.add, [pa_r, f32a_r], [mfm_r])
        held.clear()
        g5 = f32b[:, 0:40].rearrange("p (a b) -> p a b", b=5)
        K.dma(SP, g5, gmix5_in[l], (), [f32b_r], f32b_r)
        K.stt(AB[:, 0, :, :], mfm[:, 8:16, :], 1.0, g5, ALU.add, ALU.mult, [mfm_r, f32b_r], [AB_r])
        g5b = f32c[:, 0:40].rearrange("p (a b) -> p a b", b=5)
        K.dma(SP, g5b, gmlp5_in[l], (), [f32c_r], f32c_r)
        K.stt(AB[:, 1, :, :], mfm[:, 32:40, :], 1.0, g5b, ALU.add, ALU.mult, [mfm_r, f32c_r], [AB_r])
        for which in range(2):
            for half in range(2):
                pg, pg_r = nps()
                K.mm(pg[:, :], cst[0:5, 6, :], mrow_t[which][0:5, half * 512:(half + 1) * 512], True, True,
                     [cst_r, mrow_r[which]], [pg_r])
                K.cp(gabc[:, half * 512:(half + 1) * 512], pg[:, :], [pg_r], [gabc_r])
                pg2, pg2_r = nps()
                K.mm(pg2[0:NS, :], cst[0:5, 7, 0:NS], mrow_t[which][0:5, half * 512:(half + 1) * 512], True, True,
                     [cst_r, mrow_r[which]], [pg2_r])
                K.cp(gas[:, half * 512:(half + 1) * 512], pg2[0:NS, :], [pg2_r], [gas_r])
            K.dma(SP, gasc[l, which, 0:128, :], gabc[:], [gabc_r], [gasc_r[l][which]], gabc_r)
            K.dma(SP, gasc[l, which, 128:192, :], gas[:], [gas_r], [gasc_r[l][which]], gas_r)

    gasc_r = [[K.res(f"gasc{l}{w}") for w in range(2)] for l in range(NL)]

    def load_gates(l, which):
        K.dma(SP, gabc[:], gasc[l, which, 0:128, :], [gasc_r[l][which]], [gabc_r], gabc_r)
        K.dma(SP, gas[:], gasc[l, which, 128:192, :], [gasc_r[l][which]], [gas_r], gas_r)

    sm = {"i": 0}

    def norm_tile(xtile, xres, np_, slot):
        i = sm["i"] % 8
        sm["i"] += 1
        ss = small[0:np_, 2 * i:2 * i + 1]
        rr = small[0:np_, 2 * i + 1:2 * i + 2]
        sr = small_r[i]
        junk = xsb[0:np_, slot, :]
        K.actf(junk, xtile[0:np_, :], AF.Square, [xres], [xsb_r[slot], sr], accum_out=ss)
        K.actf(rr, ss, AF.Ln, [sr], [sr], bias=float(EPS), scale=1.0 / D)
        K.actf(rr, rr, AF.Exp, [sr], [sr], scale=-0.5)
        K.ts(xsb[0:np_, slot, :], xtile[0:np_, :], rr, None, ALU.mult, None, [xres, sr], [xsb_r[slot]])

    def transpose_mod(slots, np_, which, bsel):
        nt = len(slots) * np_
        for kp in range(4):
            pb, pb_r = npb()
            for kk in range(2):
                k = kp * 2 + kk
                for ti, sl in enumerate(slots):
                    K.tr(pb[:, kk * 512 + ti * np_: kk * 512 + (ti + 1) * np_], xsb[0:np_, sl, k * 128:(k + 1) * 128],
                         ident[0:np_, 0:np_], [xsb_r[sl], cstb_r], [pb_r])
            for kk in range(2):
                k = kp * 2 + kk
                for (c0, c1, b) in bsel(nt):
                    K.actf(hT[:, k, c0:c1], pb[:, kk * 512 + c0: kk * 512 + c1], AF.Identity, [pb_r, AB_r, mfm_r],
                           [hT_r], bias=mfm[:, (0 if which == 0 else 24) + k, b:b + 1], scale=AB[:, which, k, b:b + 1])

    rti = {"i": 0}

    def rot_start(pp, pp_r, dst, dst_r, cs, cs_r, nt):
        xb, xb_r = rt_b[rti["i"] % 2]
        rti["i"] += 1
        K.actf(xb[:, 0:nt], pp[:, 0:nt], AF.Copy, [pp_r], [xb_r])
        return (xb, xb_r, dst, dst_r, cs, cs_r, nt)

    def rot_finish(pend):
        (xb, xb_r, dst, dst_r, cs, cs_r, nt) = pend
        p2, p2_r = nps()
        K.mm(p2[:, 0:nt], Rt, xb[:, 0:nt], True, True, [cstb_r, xb_r], [p2_r])
        K.tt(f32b[:, 0:nt], xb[:, 0:nt], cs[:, 0, 0:nt], ALU.mult, [xb_r, cs_r], [f32b_r])
        K.tt(f32a[:, 0:nt], p2[:, 0:nt], cs[:, 1, 0:nt], ALU.mult, [p2_r, cs_r], [f32a_r])
        K.tt(dst, f32b[:, 0:nt], f32a[:, 0:nt], ALU.add, [f32b_r, f32a_r], [dst_r])

    def proj_block(blk, nt, cs, cs_r):
        pp, pp_r = nps()
        for kk in range(8):
            K.mm(pp[:, 0:nt], w_in_sb[:, kk, blk * 128:(blk + 1) * 128], hT[:, kk, 0:nt], kk == 0, kk == 7,
                 [wM_r, hT_r], [pp_r])
        if blk < 4:
            K.actf(qaT[:, blk, 0:nt], pp[:, 0:nt], AF.Copy, [pp_r], [qaT_r], scale=0.125)
        elif blk < 6:
            K.cp(kT[:, blk - 4, 128:128 + nt], pp[:, 0:nt], [pp_r], [kT_r])
        elif blk < 10:
            return rot_start(pp, pp_r, qrT[:, blk - 6, 0:nt], qrT_r, cs, cs_r, nt)
        elif blk < 14:
            return rot_start(pp, pp_r, krT[:, blk - 10, 0:nt], krT_r, cs, cs_r, nt)
        else:
            K.actf(grT[:, blk - 14, 0:nt], pp[:, 0:nt], AF.Silu, [pp_r], [grT_r])
        return None

    def proj_blocks(order, nt, cs, cs_r):
        pend = None
        for blk in order:
            newp = proj_block(blk, nt, cs, cs_r)
            if pend is not None:
                rot_finish(pend)
            pend = newp
        if pend is not None:
            rot_finish(pend)

    def project_fm(nt, cs, cs_r):
        proj_blocks(list(range(18)), nt, cs, cs_r)

    pbi = {"i": 0}

    def att_scores(q0, nq, blocks):
        Ps = []
        for (nk, kT_fn, V_fn, E_fn, rds) in blocks:
            Pt, P_r = P_b[pbi["i"] % 4]
            pbi["i"] += 1
            for hb in range(2):
                pp, pp_r = nps()
                hp = hb
                for hh in range(4):
                    m, kv = hh, hh // 2
                    K.mm(pp[0:nk, hh * nq:(hh + 1) * nq], kT_fn(hp, kv), qaT[hp * 64:(hp + 1) * 64, m, q0:q0 + nq],
                         hh == 0, False, rds + [qaT_r], [pp_r], signal=False)
                K.mm(pp[0:nk, 0:4 * nq].rearrange("p (h q) -> p h q", h=4), ident[0:nk, 0:nk], E_fn(hb), False, True,
                     [cstb_r, E_r, Es0_r, Es1_r], [pp_r], signal=True)
                K.actf(Pt[0:nk, hb * 4:(hb + 1) * 4, 0:nq], pp[0:nk, 0:4 * nq].rearrange("p (h q) -> p h q", h=4),
                       AF.Exp, [pp_r], [P_r])
            Ps.append((Pt, P_r))
        return (q0, nq, blocks, Ps)

    def att_pv(st):
        q0, nq, blocks, Ps = st
        po, po_r = nps()
        pd, pd_r = nps()
        nb = len(blocks)
        for (pt_, lfn, pr_) in ((po, 0, po_r), (pd, 1, pd_r)):
            for m in range(4):
                kv = m // 2
                cnt = 0
                for bi, (nk, kT_fn, V_fn, E_fn, rds) in enumerate(blocks):
                    for hp in range(2):
                        lhs = V_fn(kv, hp) if lfn == 0 else opad[hp][0:nk, :]
                        K.mm(pt_[:, m * nq:(m + 1) * nq], lhs, Ps[bi][0][0:nk, hp * 4 + m, 0:nq], cnt == 0,
                             cnt == 2 * nb - 1, rds + [Ps[bi][1], cstb_r], [pr_],
                             signal=(m == 3 and cnt == 2 * nb - 1))
                        cnt += 1
        for m in range(4):
            K.actf(f32b[:, m * nq:(m + 1) * nq], pd[:, m * nq:(m + 1) * nq], AF.Ln, [pd_r, esink_r], [f32b_r],
                   bias=esink[:, m:m + 1])
        K.actf(f32b[:, 0:4 * nq], f32b[:, 0:4 * nq], AF.Exp, [f32b_r], [f32b_r], scale=-1.0)
        K.tt(oT[:, :, q0:q0 + nq], po[:, 0:4 * nq].rearrange("p (m q) -> p m q", m=4),
             f32b[:, 0:4 * nq].rearrange("p (m q) -> p m q", m=4), ALU.mult, [po_r, f32b_r], [oT_r])

    rci = {"i": 0}

    def ret_front(c0, n, vr_fn, vr_reads, rc, Sb_in, Sb_out):
        maskT = rc[0:n, 0, 0:4 * n].rearrange("p (h q) -> p h q", h=4)
        qdec = rc[:, 1, 0:4 * n].rearrange("p (h q) -> p h q", h=4)
        kdec = rc[0:n, 2, :].rearrange("p (h q) -> p h q", h=4)
        i = rci["i"] % 2
        rci["i"] += 1
        ktl, ktl_r = kt_b[i]
        sct, sct_r = sc_b[i]
        pb, pb_r = npb()
        for h in range(4):
            K.tr(pb[0:n, h * 128:(h + 1) * 128], krT[:, h, c0:c0 + n], ident, [krT_r, cstb_r], [pb_r],
                 signal=(h == 3))
        K.tt(ktl[0:n, :, :], pb[0:n, 0:512].rearrange("p (h q) -> p h q", h=4), kdec, ALU.mult, [pb_r, rcc_r],
             [ktl_r])
        ps_, ps_r = nps()
        for h in range(4):
            K.mm(ps_[0:n, h * n:(h + 1) * n], krT[:, h, c0:c0 + n], qrT[:, h, c0:c0 + n], True, True,
                 [krT_r, qrT_r], [ps_r], signal=(h == 3))
        K.tt(sct[0:n, :, 0:n], ps_[0:n, 0:4 * n].rearrange("p (h q) -> p h q", h=4), maskT, ALU.mult,
             [ps_r, rcc_r], [sct_r])
        return dict(c0=c0, n=n, vr_fn=vr_fn, vr_reads=vr_reads, rc=rc, Sb_in=Sb_in, Sb_out=Sb_out, ktl=ktl,
                    ktl_r=ktl_r, sct=sct, sct_r=sct_r, qdec=qdec)

    def ret_pre(st):
        c0, n = st["c0"], st["n"]
        K.tt(qtl[:, :, 0:n], qrT[:, :, c0:c0 + n], st["qdec"], ALU.mult, [qrT_r, rcc_r], [qt_r])

    def ret_mid(st):
        c0, n, vr_fn, vr_reads, rc = st["c0"], st["n"], st["vr_fn"], st["vr_reads"], st["rc"]
        Sb_in, Sb_out, ktl, ktl_r, sct, sct_r = st["Sb_in"], st["Sb_out"], st["ktl"], st["ktl_r"], st["sct"], st["sct_r"]
        Gc = rc[:, 3, :].rearrange("p (h q) -> p h q", h=4)
        pdl, pdl_r = nps()
        for h in range(4):
            K.mm(pdl[:, h * 128:(h + 1) * 128], ktl[0:n, h, :], vr_fn(h), True, True, vr_reads + [ktl_r], [pdl_r],
                 signal=(h == 3))
        po, po_r = nps()
        for h in range(4):
            K.mm(po[:, h * n:(h + 1) * n], vr_fn(h), sct[0:n, h, 0:n], True, False, vr_reads + [sct_r], [po_r],
                 signal=False)
            K.mm(po[:, h * n:(h + 1) * n], Sb_in[0][:, h, :], qtl[:, h, 0:n], False, True, [Sb_in[1], qt_r], [po_r],
                 signal=(h == 3))
        K.actf(sq_f[:, 0:4 * n], po[:, 0:4 * n], AF.Square, [po_r], [sq_r])
        K.actf(f32c[:, 0:4 * n], po[:, 0:4 * n], AF.Copy, [po_r], [f32c_r])
        K.tt(S[:], S[:], Gc, ALU.mult, [S_r, rcc_r], [S_r])
        K.tt(S[:], S[:], pdl[:, :].rearrange("p (h q) -> p h q", h=4), ALU.add, [S_r, pdl_r], [S_r])
        K.actf(Sb_out[0], S[:], AF.Copy, [S_r], [Sb_out[1]])

    def ret_back(st):
        c0, n = st["c0"], st["n"]
        pn, pn_r = nps()
        K.mm(pn[:, 0:4 * n], ones_bf, sq_f[:, 0:4 * n], True, True, [cstb_r, sq_r], [pn_r])
        K.actf(f32a[:, 0:4 * n], pn[:, 0:4 * n], AF.Ln, [pn_r], [f32a_r], bias=float(EPS), scale=1.0 / 128.0)
        K.actf(f32a[:, 0:4 * n], f32a[:, 0:4 * n], AF.Exp, [f32a_r], [f32a_r], scale=-0.5)
        K.tt(f32a[:, 0:4 * n], f32c[:, 0:4 * n], f32a[:, 0:4 * n], ALU.mult, [f32c_r, f32a_r], [f32a_r])
        K.tt(mx[:, :, c0:c0 + n], f32a[:, 0:4 * n].rearrange("p (h q) -> p h q", h=4), grT[:, :, c0:c0 + n],
             ALU.mult, [f32a_r, grT_r], [mx_r])

    def attention(q0, nq, blocks):
        att_pv(att_scores(q0, nq, blocks))

    def retention(c0, n, vr_fn, vr_reads, rc, Sb_in, Sb_out):
        st = ret_front(c0, n, vr_fn, vr_reads, rc, Sb_in, Sb_out)
        ret_pre(st)
        ret_mid(st)
        ret_back(st)

    def fold_gate_into_w_out():
        for kk in range(8):
            K.tt(w_out_sb[:, kk, :], w_out_sb[:, kk, :], gabc[:, :], ALU.mult, [wM_r, gabc_r], [wM_r])

    def out_proj(t0, np_, xres_tile, xres_r, gate, gate_r, folded=False, halves=(0, 1)):
        for half in halves:
            pp, pp_r = nps()
            for m in range(4):
                K.mm(pp[0:np_, :], oT[:, m, t0:t0 + np_], w_out_sb[:, m, half * 512:(half + 1) * 512], m == 0, False,
                     [oT_r, wM_r], [pp_r], signal=False)
            for h in range(4):
                K.mm(pp[0:np_, :], mx[:, h, t0:t0 + np_], w_out_sb[:, 4 + h, half * 512:(half + 1) * 512], False,
                     h == 3, [mx_r, wM_r], [pp_r])
            if folded:
                K.tt(xres_tile[0:np_, half * 512:(half + 1) * 512], pp[0:np_, :],
                     xres_tile[0:np_, half * 512:(half + 1) * 512], ALU.add, [pp_r, xres_r], [xres_r])
                continue
            K.tt(f32a[0:np_, :], pp[0:np_, :], gate[0:np_, half * 512:(half + 1) * 512], ALU.mult, [pp_r, gate_r],
                 [f32a_r])
            K.tt(xres_tile[0:np_, half * 512:(half + 1) * 512], xres_tile[0:np_, half * 512:(half + 1) * 512],
                 f32a[0:np_, :], ALU.add, [xres_r, f32a_r], [xres_r])

    def bsel_prompt(nt):
        return [(0, nt, 0)]

    def bsel_sample(nt):
        return [(b * 16, (b + 1) * 16, 1 + b) for b in range(NSB)]

    def phase_M_prompt(l, xin, xout, xout_r):
        NG = SEQ // TM
        xin_r = xres_store[id(xin)] if id(xin) in xres_store else None

        def rd(g):
            return [xin_r[g]] if xin_r is not None else []

        def nload(g, ts_):
            for t in ts_:
                row = g * TM + t * 128
                sl = t % 2
                K.dma(SP, xt[sl][:], xin[row:row + 128, :], rd(row // 128), [xt_r[sl]], xt_r[sl])

        def ncomp(g, ts_):
            for t in ts_:
                norm_tile(xt[t % 2], xt_r[t % 2], 128, t)

        nload(0, [0, 1])
        ncomp(0, [0, 1])
        nload(0, [2, 3])
        ncomp(0, [2, 3])
        if NG > 1:
            nload(1, [0, 1])
        chk("m0")
        transpose_mod([0, 1, 2, 3], 128, 0, bsel_prompt)
        chk("m1")
        def load_cs(g):
            cs, cs_r = cs_b[g % 2]
            K.dma(PL, cs[:, 0, :], cos_in[:, g * TM:(g + 1) * TM], (), [cs_r], cs_r)
            K.dma(PL, cs[:, 1, :], sin_in[:, g * TM:(g + 1) * TM], (), [cs_r], cs_r)

        load_cs(0)
        project_fm(TM, *cs_b[0])
        for g in range(NG):
            chk("m2")
            def vproj_a(t, g=g):
                pv, pv_r = nps()
                for kk in range(8):
                    K.mm(pv[:, 0:128], hT[:, kk, t * 128:(t + 1) * 128], w_in_sb[:, kk, 2304:2432], kk == 0, kk == 7,
                         [hT_r, wM_r], [pv_r])
                for hp in range(2):
                    K.cp(Vp[:, 1 + t, :, hp, hp * 64:(hp + 1) * 64],
                         pv[:, 0:128].rearrange("p (v e) -> p v e", v=2), [pv_r], [Vp_r])
                if g == NG - 1 and t == 3:
                    K.cp(f32a[:, 0:128], pv[:, 0:128], [pv_r], [f32a_r])
                    K.dma(SP, wv_p[l], f32a[:, 0:128], [f32a_r], [out_r], f32a_r)
                    pk, pk_r = nps()
                    for kk in range(8):
                        K.mm(pk[:, 0:256], hT[:, kk, 384:512], w_in_sb[:, kk, 512:768], kk == 0, kk == 7,
                             [hT_r, wM_r], [pk_r])
                    K.cp(f32b[:, 0:128].rearrange("p (v e) -> p v e", v=2),
                         pk[:, 0:256].rearrange("p (v e) -> p v e", v=2)[:, :, 0:64], [pk_r], [f32b_r])
                    K.dma(SP, wk_p[l], f32b[:, 0:128], [f32b_r], [out_r], f32b_r)

            def vproj_b(t):
                pw, pw_r = nps()
                for kk in range(8):
                    K.mm(pw[:, :], hT[:, kk, t * 128:(t + 1) * 128], w_in_sb[:, kk, 2432:2944], kk == 0, kk == 7,
                         [hT_r, wM_r], [pw_r])
                K.cp(vr[:, t, :], pw[:, :], [pw_r], [vr_r])

            vproj_a(0)
            vproj_b(0)
            fq = []
            vdone = [True, False, False, False]
            for t in (1, 2, 3):
                fq.append(("va", t, (lambda t=t: vproj_a(t))))
                fq.append(("vb", t, (lambda t=t: vproj_b(t))))

            def pop(n=1):
                for _ in range(n):
                    if not fq:
                        return
                    kind, t_, fn = fq.pop(0)
                    fn()
                    if kind == "vb":
                        vdone[t_] = True

            def need_v(t_):
                while not vdone[t_]:
                    pop()

            chk("m3")
            if g + 1 < NG:
                ncomp(g + 1, [0, 1])
                nload(g + 1, [2, 3])
            ast = [None] * 4
            rst = [None] * 4

            def alpha(t, g=g):
                blocks = []
                if not (g == 0 and t == 0):
                    blocks.append((128,
                                   (lambda hp, kv, t=t: kT[hp * 64:(hp + 1) * 64, kv, t * 128:(t + 1) * 128]),
                                   (lambda kv, hp, t=t: Vp[:, t, kv, hp, :]),
                                   (lambda hb: E[:, hb * 4:(hb + 1) * 4, 128:256]), [kT_r, Vp_r]))
                blocks.append((128,
                               (lambda hp, kv, t=t: kT[hp * 64:(hp + 1) * 64, kv, (t + 1) * 128:(t + 2) * 128]),
                               (lambda kv, hp, t=t: Vp[:, t + 1, kv, hp, :]),
                               (lambda hb: E[:, hb * 4:(hb + 1) * 4, 0:128]), [kT_r, Vp_r]))
                ast[t] = att_scores(t * 128, 128, blocks)
                ci = (g * 4 + t) % 2
                rst[t] = ret_front(t * 128, 128, (lambda h, t=t: vr[:, t, h * 128:(h + 1) * 128]), [vr_r], rc128,
                                   Sbf_b[1 - ci], Sbf_b[ci])

            def beta(t):
                need_v(t)
                ret_pre(rst[t])
                att_pv(ast[t])
                pop()
                ret_mid(rst[t])

            def gamma1(t):
                ret_back(rst[t])

            def gamma2_a(t, g=g):
                row = g * TM + t * 128
                sl = 2 + (t % 2)
                K.dma(SP, xt[sl][:], xin[row:row + 128, :], rd(row // 128), [xt_r[sl]], xt_r[sl])
                out_proj(t * 128, 128, xt[sl], xt_r[sl], gabc, gabc_r, folded=True, halves=(0,))

            def gamma2_b(t, g=g):
                row = g * TM + t * 128
                sl = 2 + (t % 2)
                out_proj(t * 128, 128, xt[sl], xt_r[sl], gabc, gabc_r, folded=True, halves=(1,))
                K.dma(PL, xout[row:row + 128, :], xt[sl][:], [xt_r[sl]], [xout_r[row // 128]], xt_r[sl])

            pendq = {"p": None}

            def qblock(blk, ncs):
                def run():
                    newp = proj_block(blk, TM, *ncs)
                    if pendq["p"] is not None:
                        rot_finish(pendq["p"])
                    pendq["p"] = newp
                return run

            for i in range(7):
                if 0 <= i - 3 < 4:
                    fq.append(("g2", i - 3, (lambda t=i - 3: gamma2_a(t))))
                    fq.append(("g2", i - 3, (lambda t=i - 3: gamma2_b(t))))
                if g + 1 < NG and i in (5, 6):
                    ncs = cs_b[(g + 1) % 2]
                    order = [0, 1, 2, 3, 10, 11, 12, 13] if i == 5 else [6, 7, 8, 9, 4, 5, 14, 15, 16, 17]
                    for blk in order:
                        fq.append(("pj", blk, qblock(blk, ncs)))
                if 0 <= i - 2 < 4:
                    gamma1(i - 2)
                pop()
                if 0 <= i - 1 < 4:
                    beta(i - 1)
                pop()
                if i < 4:
                    alpha(i)
                pop(2 if len(fq) > 4 else 1)
                if i == 2 and g + 1 < NG:
                    ncomp(g + 1, [2, 3])
                    if g + 2 < NG:
                        nload(g + 2, [0, 1])
                if i == 4:
                    K.cp(kT[:, :, 0:128], kT[:, :, 512:640], [kT_r], [kT_r])
                    K.cp(Vp[:, 0, :, :, :], Vp[:, 4, :, :, :], [Vp_r], [Vp_r])
                    if g + 1 < NG:
                        load_cs(g + 1)
                        transpose_mod([0, 1, 2, 3], 128, 0, bsel_prompt)
                chk(f"L{i}")
            while fq:
                pop()
            if pendq["p"] is not None:
                rot_finish(pendq["p"])
                pendq["p"] = None
            chk("m5")
        K.dma(SP, st_p[l], S[:], [S_r], [out_r], S_r)

    def mlp_group(ntiles, np_, which_gate, gate, gate_r, xtiles, bsel, final, store_fn):
        nt = ntiles * np_
        for f in range(32):
            pp, pp_r = nps()
            for kk in range(8):
                K.mm(pp[:, 0:nt], w_up_sb[:, kk, f * 128:(f + 1) * 128], hT[:, kk, 0:nt], kk == 0, kk == 7,
                     [wF_r, hT_r], [pp_r])
            fb, fb_r = (f32a, f32a_r) if f % 2 == 0 else (f32b, f32b_r)
            K.actf(fb[:, 0:nt], pp[:, 0:nt], AF.Relu, [pp_r], [fb_r])
            K.tt(uT[:, f, 0:nt], fb[:, 0:nt], fb[:, 0:nt], ALU.mult, [fb_r], [uT_r])

    def mlp_down(ti, np_, xtile, xtile_r, gate, gate_r, final, store_fn, split=False):
        for half in range(2):
            pp, pp_r = nps()
            for f in range(32):
                K.mm(pp[0:np_, :], uT[:, f, ti * np_:(ti + 1) * np_], w_dn_sb[:, f, half * 512:(half + 1) * 512],
                     f == 0, f == 31, [uT_r, wD_r], [pp_r])
            K.tt(f32c[0:np_, :], pp[0:np_, :], gate[0:np_, half * 512:(half + 1) * 512], ALU.mult, [pp_r, gate_r],
                 [f32c_r])
            K.tt(xtile[0:np_, half * 512:(half + 1) * 512], xtile[0:np_, half * 512:(half + 1) * 512],
                 f32c[0:np_, :], ALU.add, [xtile_r, f32c_r], [xtile_r])
        def fin():
            if final:
                i = sm["i"] % 8
                sm["i"] += 1
                ss = small[0:np_, 2 * i:2 * i + 1]
                rr = small[0:np_, 2 * i + 1:2 * i + 2]
                sr = small_r[i]
                K.actf(f32a[0:np_, :], xtile[0:np_, 0:512], AF.Square, [xtile_r], [f32a_r, sr], accum_out=ss)
                ss2 = small[0:np_, 32 + i:33 + i]
                K.actf(f32a[0:np_, :], xtile[0:np_, 512:1024], AF.Square, [xtile_r], [f32a_r, sr], accum_out=ss2)
                K.tt(ss, ss, ss2, ALU.add, [sr], [sr])
                K.actf(rr, ss, AF.Ln, [sr], [sr], bias=float(EPS), scale=1.0 / D)
                K.actf(rr, rr, AF.Exp, [sr], [sr], scale=-0.5)
                K.stt(xtile[0:np_, :], xtile[0:np_, :], rr, gfin[0:np_, :], ALU.mult, ALU.mult,
                      [xtile_r, sr, gfin_r], [xtile_r])
            store_fn()
        if split:
            return fin
        fin()
        return None

    def phase_F_prompt(l, xin, xin_r, xout, xout_r, final):
        NG = SEQ // TF

        def load_group(g):
            for t in range(2):
                row = g * TF + t * 128
                sl = (g % 2) * 2 + t
                K.dma(SP, xt[sl][:], xin[row:row + 128, :], [xin_r[row // 128]], [xt_r[sl]], xt_r[sl])

        def norm_group(g):
            for t in range(2):
                sl = (g % 2) * 2 + t
                norm_tile(xt[sl], xt_r[sl], 128, sl)

        load_group(0)
        norm_group(0)
        transpose_mod([0, 1], 128, 1, bsel_prompt)
        for g in range(NG):
            if g + 1 < NG:
                load_group(g + 1)
            mlp_group(2, 128, 1, gabc, gabc_r, None, bsel_prompt, final, None)
            if g + 1 < NG:
                norm_group(g + 1)
            for t in range(2):
                row = g * TF + t * 128
                sl = (g % 2) * 2 + t

                def store(row=row, sl=sl):
                    if final:
                        K.dma(PL, y_p[row:row + 128, :], xt[sl][:], [xt_r[sl]], [out_r], xt_r[sl])
                    else:
                        K.dma(PL, xout[row:row + 128, :], xt[sl][:], [xt_r[sl]], [xout_r[row // 128]], xt_r[sl])
                fin0 = mlp_down(t, 128, xt[sl], xt_r[sl], gabc, gabc_r, final, store, split=True)
                if t == 0 and g + 1 < NG:
                    sl0 = ((g + 1) % 2) * 2
                    transpose_mod([sl0, sl0 + 1], 128, 1, bsel_prompt)
                fin0()

    def phase_M_sample(l, xin, xin_r, xout, xout_r):
        K.dma(SP, xt[0][0:NS, :], xin, xin_r, [xt_r[0]], xt_r[0])
        norm_tile(xt[0], xt_r[0], NS, 0)
        transpose_mod([0], NS, 0, bsel_sample)
        cs, cs_r = cs_b[0]
        K.dma(PL, cs[:, 0, 0:NS], coss_in, (), [cs_r], cs_r)
        K.dma(PL, cs[:, 1, 0:NS], sins_in, (), [cs_r], cs_r)
        project_fm(NS, cs, cs_r)
        for b in range(NSB):
            c0 = b * 16
            pv, pv_r = nps()
            for kk in range(8):
                K.mm(pv[0:16, 0:128], hT[:, kk, c0:c0 + 16], w_in_sb[:, kk, 2304:2432], kk == 0, kk == 7,
                     [hT_r, wM_r], [pv_r])
            K.memset(Vp[:, 1, :, :, :], 0.0, [Vp_r])
            for hp in range(2):
                K.cp(Vp[0:16, 1, :, hp, hp * 64:(hp + 1) * 64], pv[0:16, 0:128].rearrange("p (v e) -> p v e", v=2),
                     [pv_r], [Vp_r])
            K.cp(f32c[0:16, 0:128], pv[0:16, 0:128], [pv_r], [f32c_r])
            K.dma(SP, wv_s[l, b, 112:128, :], f32c[0:16, 0:128], [f32c_r], [out_r], f32c_r)
            pk, pk_r = nps()
            for kk in range(8):
                K.mm(pk[0:16, 0:256], hT[:, kk, c0:c0 + 16], w_in_sb[:, kk, 512:768], kk == 0, kk == 7,
                     [hT_r, wM_r], [pk_r])
            K.cp(f32b[0:16, 0:128].rearrange("p (v e) -> p v e", v=2),
                 pk[0:16, 0:256].rearrange("p (v e) -> p v e", v=2)[:, :, 0:64], [pk_r], [f32b_r])
            K.dma(SP, wk_s[l, b, 112:128, :], f32b[0:16, 0:128], [f32b_r], [out_r], f32b_r)
            pw, pw_r = nps()
            for kk in range(8):
                K.mm(pw[0:16, :], hT[:, kk, c0:c0 + 16], w_in_sb[:, kk, 2432:2944], kk == 0, kk == 7,
                     [hT_r, wM_r], [pw_r])
            K.cp(vr[0:16, 0, :], pw[0:16, :], [pw_r], [vr_r])
            K.dma(SP, wk_s[l, b, 0:112, :], ck_in[l, b, 16:128, :], (), [out_r], out_r)
            K.dma(SP, wv_s[l, b, 0:112, :], cv_in[l, b, 16:128, :], (), [out_r], out_r)
            K.dma(SP, xt[2][:, 0:128], ck_in[l, b], (), [xt_r[2]], xt_r[2])
            K.dma(SP, xt[2][:, 128:256], cv_in[l, b], (), [xt_r[2]], xt_r[2])
            ckd, ckd_r = rt_b[0]
            for dup in range(2):
                K.cp(ckd[:, 0:256].rearrange("p (v d e) -> p v d e", v=2, d=2)[:, :, dup, :],
                     xt[2][:, 0:128].rearrange("p (v e) -> p v e", v=2), [xt_r[2]], [ckd_r])
            pb, pb_r = npb()
            for kv in range(2):
                K.tr(pb[:, kv * 128:(kv + 1) * 128], ckd[:, kv * 128:(kv + 1) * 128], ident, [ckd_r, cstb_r], [pb_r],
                     signal=(kv == 1))
            K.cp(kT[:, :, 0:128], pb[:, 0:256].rearrange("p (v t) -> p v t", v=2), [pb_r], [kT_r])
            for hp in range(2):
                K.cp(Vp[:, 0, :, hp, hp * 64:(hp + 1) * 64], xt[2][:, 128:256].rearrange("p (v e) -> p v e", v=2),
                     [xt_r[2]], [Vp_r])
            blocks = [
                (128, (lambda hp, kv: kT[hp * 64:(hp + 1) * 64, kv, 0:128]),
                 (lambda kv, hp: Vp[:, 0, kv, hp, :]),
                 (lambda hb: Es0[:, hb * 4:(hb + 1) * 4, :]), [kT_r, Vp_r]),
                (16, (lambda hp, kv, c0=c0: kT[hp * 64:(hp + 1) * 64, kv, 128 + c0:128 + c0 + 16]),
                 (lambda kv, hp: Vp[0:16, 1, kv, hp, :]),
                 (lambda hb: Es1[0:16, hb * 4:(hb + 1) * 4, :]), [kT_r, Vp_r]),
            ]
            attention(c0, 16, blocks)
            K.dma(SP, S[:], st_in[l, b], (), [S_r], S_r)
            K.actf(Sbf_b[0][0], S[:], AF.Copy, [S_r], [Sbf_b[0][1]])
            retention(c0, 16, (lambda h: vr[0:16, 0, h * 128:(h + 1) * 128]), [vr_r], rc16, Sbf_b[0], Sbf_b[1])
            K.dma(SP, st_s[l, b], S[:], [S_r], [out_r], S_r)
        K.dma(SP, xt[3][0:NS, :], xin, xin_r, [xt_r[3]], xt_r[3])
        out_proj(0, NS, xt[3], xt_r[3], gas, gas_r)
        K.dma(PL, xout, xt[3][0:NS, :], [xt_r[3]], [xout_r], xt_r[3])

    def phase_F_sample(l, xin, xin_r, xout, xout_r, final):
        K.dma(SP, xt[0][0:NS, :], xin, [xin_r], [xt_r[0]], xt_r[0])
        norm_tile(xt[0], xt_r[0], NS, 0)
        transpose_mod([0], NS, 1, bsel_sample)
        mlp_group(1, NS, 1, gas, gas_r, None, bsel_sample, final, None)

        def store():
            if final:
                K.dma(PL, y_s, xt[0][0:NS, :], [xt_r[0]], [out_r], xt_r[0])
            else:
                K.dma(PL, xout, xt[0][0:NS, :], [xt_r[0]], [xout_r], xt_r[0])
        mlp_down(0, NS, xt[0], xt_r[0], gas, gas_r, final, store)

    out_r = K.res("outputs")
    xres_store = {}
    xsc_r = [[K.res(f"xsc{i}_{j}") for j in range(SEQ // 128)] for i in range(2)]
    xssc_r = [K.res("xssc0"), K.res("xssc1")]
    xres_store[id(xsc[0])] = xsc_r[0]
    xres_store[id(xsc[1])] = xsc_r[1]

    def chk(tag):
        if stop == tag:
            raise _Stop()

    K.barrier()
    try:
        if stop in ("s1", "s2", "s3", "s4"):
            raise _Stop()
        chk("setup")
        for l in range(NL):
            ada(l)
            K.barrier()
            chk(f"ada{l}")
            load_weights_M(l)
            load_gates(l, 0)
            fold_gate_into_w_out()
            chk(f"wM{l}")
            xin_p = xp if l == 0 else xsc[1]
            phase_M_prompt(l, xin_p, xsc[0], xsc_r[0])
            chk(f"Mp{l}")
            K.dma(PL, w_out_sb, wout_in[l].rearrange("(k p) n -> p k n", p=128), (), [wM_r], wM_r)
            xin_s = xs_in if l == 0 else xssc[1]
            phase_M_sample(l, xin_s, [] if l == 0 else [xssc_r[1]], xssc[0], xssc_r[0])
            K.barrier()
            chk(f"Ms{l}")
            load_weights_F(l)
            load_gates(l, 1)
            final = (l == NL - 1)
            phase_F_prompt(l, xsc[0], xsc_r[0], xsc[1], xsc_r[1], final)
            chk(f"Fp{l}")
            phase_F_sample(l, xssc[0], xssc_r[0], xssc[1], xssc_r[1], final)
            K.barrier()
            chk(f"Fs{l}")
    except _Stop:
        pass
    K.barrier()
    es.close()
    return nc


_PROG = None
_STOP = None


def _consts(SEQ=SEQ):
    lg = np.log(1.0 - 2.0 ** (-5.0 - np.arange(4, dtype=np.float64)))
    sc = 128.0 ** -0.5

    def retc(n):
        out = np.zeros((128, 4, 512), np.float32)
        i = np.arange(n)
        diff = (i[None, :] - i[:, None]).astype(np.float64)
        for h in range(4):
            m = np.where(diff >= 0, np.exp(np.maximum(diff, 0) * lg[h]), 0.0) * sc
            out[0:n, 0, h * n:(h + 1) * n] = m
            out[:, 1, h * n:(h + 1) * n] = np.exp((i + 1.0) * lg[h])[None, :]
            out[0:n, 2, h * 128:(h + 1) * 128] = (np.exp((n - 1.0 - i) * lg[h]) * sc)[:, None]
            out[:, 3, h * 128:(h + 1) * 128] = np.exp(n * lg[h])
        return out

    cst = np.zeros((128, 12, 128), np.float32)
    cst[:, 0, :] = np.eye(128)
    for d in range(64):
        cst[d + 64, 1, d] = -1.0
        cst[d, 1, d + 64] = 1.0
    cst[:, 2, :] = np.eye(128)[::-1]
    cst[:, 3, :] = 1.0
    cst[:, 4, 0:64] = 1.0
    cst[:, 5, 64:128] = 1.0
    cst[0, 6, :] = 1.0
    for b in range(NSB):
        cst[1 + b, 7, b * 16:(b + 1) * 16] = 1.0

    def bucket(rel):
        nb = 16
        n = -rel
        ret = np.where(n < 0, nb, 0)
        n = np.abs(n)
        me = 8
        nf = np.maximum(n, 1).astype(np.float32)
        large = me + (np.log(nf / np.float32(me)).astype(np.float32) / np.float32(math.log(128 / me))
                      * np.float32(nb - me)).astype(np.int32)
        large = np.minimum(large, nb - 1)
        return ret + np.where(n < me, n, large)

    s = np.arange(384)
    bk = bucket(127 - s)
    oh = np.zeros((32, 384), np.float32)
    oh[bk, s] = 1.0
    j = np.arange(128)[:, None]
    ip = np.arange(256)[None, :]
    dq = ip // 64 - j // 64
    valid = ((dq >= 0) & (dq <= 2)).astype(np.float32)
    valid = np.ascontiguousarray(np.broadcast_to(valid[:, None, :], (128, 8, 256)))

    inv = (1.0 / (10000.0 ** (np.arange(64, dtype=np.float32) / np.float32(64)))).astype(np.float32)

    def cs(pos):
        ang = (pos.astype(np.float32)[:, None] * inv[None, :]).astype(np.float32)
        c = np.cos(ang.astype(np.float64)).astype(np.float32).T
        s_ = np.sin(ang.astype(np.float64)).astype(np.float32).T
        return np.ascontiguousarray(np.concatenate([c, c], 0)), np.ascontiguousarray(np.concatenate([s_, s_], 0))

    cosT, sinT = cs(np.arange(SEQ))
    c16, s16 = cs(4096 + np.arange(16))
    cosTs = np.ascontiguousarray(np.tile(c16, (1, NSB)))
    sinTs = np.ascontiguousarray(np.tile(s16, (1, NSB)))
    return dict(cst128=cst, onehot=oh, valid=valid, retc128=retc(128), retc16=retc(16), cosT=cosT, sinT=sinT,
                cosTs=cosTs, sinTs=sinTs)


def kernel(x_prompt, x_sample, c_prompt, c_sample, cache_win_k, cache_win_v, state_ret, g_mix, g_mlp, w_ada, b_ada,
           w_in, w_out, att_sinks, rel_bias, w_up, w_down, g_final):
    global _PROG
    f = lambda a: np.ascontiguousarray(np.asarray(a, dtype=np.float32))
    x_prompt, x_sample, c_prompt, c_sample = f(x_prompt), f(x_sample), f(c_prompt), f(c_sample)
    cache_win_k, cache_win_v, state_ret = f(cache_win_k), f(cache_win_v), f(state_ret)
    g_mix, g_mlp, w_ada, b_ada, w_in, w_out = f(g_mix), f(g_mlp), f(w_ada), f(b_ada), f(w_in), f(w_out)
    att_sinks, rel_bias, w_up, w_down, g_final = f(att_sinks), f(rel_bias), f(w_up), f(w_down), f(g_final)
    SEQ = x_prompt.shape[1]
    if _PROG is None or _PROG[0] != (SEQ, _STOP):
        _PROG = ((SEQ, _STOP), build_program(SEQ, _STOP))
    nc = _PROG[1]
    cs = _consts(SEQ)

    def fm(v, nchunk):
        return np.ascontiguousarray(v.reshape(nchunk, 128).T)

    gmix5 = np.stack([np.repeat(fm(g_mix[l], 8)[:, :, None], 5, 2) for l in range(NL)])
    gmlp5 = np.stack([np.repeat(fm(g_mlp[l], 8)[:, :, None], 5, 2) for l in range(NL)])
    bada5 = np.stack([np.repeat(fm(b_ada[l], 48)[:, :, None], 5, 2) for l in range(NL)])
    bada_row = np.stack([np.broadcast_to(np.stack([b_ada[l, 2048:3072], b_ada[l, 5120:6144]])[None], (5, 2, D))
                         for l in range(NL)])
    gfin_bc = np.ascontiguousarray(np.broadcast_to(g_final[None, :], (128, D)))
    sk = np.zeros((NL, 128, 4), np.float32)
    for m in range(4):
        sk[:, 0:64, m] = att_sinks[:, 2 * m][:, None]
        sk[:, 64:128, m] = att_sinks[:, 2 * m + 1][:, None]
    perm = [2 * (sl % 4) + sl // 4 for sl in range(8)]
    rel_bias_perm = np.ascontiguousarray(rel_bias[:, perm])
    zero_prompt = np.zeros_like(x_prompt[0])
    bada5_z = np.ascontiguousarray(bada5).copy()
    bada5_z[:, :, :, 0] = 0.0
    bada_row_z = np.ascontiguousarray(bada_row).copy()
    bada_row_z[:, 0] = 0.0
    in_maps = []
    for c in range(8):
        b = c % 2
        sb = slice(4 * c, 4 * c + 4)
        crow = np.concatenate([c_prompt[b:b + 1] if c < 2 else np.zeros_like(c_prompt[0:1]), c_sample[sb]], 0)
        cT = np.ascontiguousarray(crow.reshape(5, 8, 128).transpose(2, 1, 0))
        m = dict(
            xp=(x_prompt[b] if c < 2 else zero_prompt), xs=np.ascontiguousarray(x_sample[sb].reshape(NS, D)), cT=cT,
            cache_k=np.ascontiguousarray(cache_win_k[:, sb].reshape(NL, NSB, 128, 128)),
            cache_v=np.ascontiguousarray(cache_win_v[:, sb].reshape(NL, NSB, 128, 128)),
            state=np.ascontiguousarray(state_ret[:, sb].transpose(0, 1, 3, 2, 4)),
            gmix5=np.ascontiguousarray(gmix5), gmlp5=np.ascontiguousarray(gmlp5), gfin_bc=gfin_bc,
            w_ada=w_ada, bada5=(np.ascontiguousarray(bada5) if c < 2 else bada5_z),
            bada_row=(np.ascontiguousarray(bada_row) if c < 2 else bada_row_z),
            w_in=w_in, w_out=w_out, w_up=w_up, w_down=w_down, sinks=sk, rel_bias=rel_bias_perm,
            w_in_p=w_in, w_out_p=w_out, w_up_p=w_up, w_down_p=w_down,
        )
        m.update(cs)
        in_maps.append(m)
    res = run_bass_kernel_spmd(nc, in_maps, core_ids=list(range(8)))
    R = res.results
    y_prompt = np.stack([R[0]["y_p"], R[1]["y_p"]]).astype(np.float32)
    y_sample = np.concatenate([R[c]["y_s"].reshape(NSB, 16, D) for c in range(8)], 0).astype(np.float32)
    wkp = np.stack([np.stack([R[b]["wk_p"][l].reshape(128, 2, 64) for b in range(2)]) for l in range(NL)])
    wvp = np.stack([np.stack([R[b]["wv_p"][l].reshape(128, 2, 64) for b in range(2)]) for l in range(NL)])
    stp = np.stack([np.stack([R[b]["st_p"][l].transpose(1, 0, 2) for b in range(2)]) for l in range(NL)])
    wks = np.concatenate([R[c]["wk_s"].reshape(NL, NSB, 128, 2, 64) for c in range(8)], 1)
    wvs = np.concatenate([R[c]["wv_s"].reshape(NL, NSB, 128, 2, 64) for c in range(8)], 1)
    sts = np.concatenate([R[c]["st_s"].transpose(0, 1, 3, 2, 4) for c in range(8)], 1)
    return (y_prompt, y_sample, wkp.astype(np.float32), wvp.astype(np.float32), stp.astype(np.float32),
            wks.astype(np.float32), wvs.astype(np.float32), sts.astype(np.float32))
```

```python
import math
from contextlib import ExitStack
import numpy as np
import concourse.bass as bass
import concourse.mybir as mybir
from concourse.bass_utils import run_bass_kernel_spmd

F32 = mybir.dt.float32
BF16 = mybir.dt.bfloat16
AF = mybir.ActivationFunctionType
ALU = mybir.AluOpType

D = 1024
SEQ = 16384
NL = 2
DFF = 4096
NSB = 4
NS = 64
EPS = 1e-6
TM = 512
TF = 256
WIN_COLS = 2944
SAME_ENG_SYNC = True


class Res:
    __slots__ = ("name", "w", "r", "dsem", "dcnt")

    def __init__(self, name):
        self.name = name
        self.w = None
        self.r = {}
        self.dsem = None
        self.dcnt = 0


class Eng:
    def __init__(self, name, eng, sem, self_ok):
        self.name = name
        self.eng = eng
        self.sem = sem
        self.cnt = 0
        self.known = {}
        self.self_ok = self_ok


class Kern:
    def __init__(self, nc, es):
        self.nc = nc
        self.es = es
        self.nsem = 0
        mk = self.newsem
        self.pe = Eng("pe", nc.tensor, mk("pe"), True)
        self.act = Eng("act", nc.scalar, mk("act"), not SAME_ENG_SYNC)
        self.dve = Eng("dve", nc.vector, mk("dve"), not SAME_ENG_SYNC)
        self.pool = Eng("pool", nc.gpsimd, mk("pool"), not SAME_ENG_SYNC)
        self.sp = Eng("sp", nc.sync, mk("sp"), True)
        self.engs = [self.pe, self.act, self.dve, self.pool, self.sp]
        self.allres = []
        self.dsems = []

    def newsem(self, name):
        self.nsem += 1
        return self.es.enter_context(self.nc.semaphore(f"s_{name}_{self.nsem}"))

    def res(self, name):
        r = Res(name)
        self.allres.append(r)
        return r

    def sbuf(self, name, shape, dt):
        return self.es.enter_context(self.nc.sbuf_tensor(name, list(shape), dt))

    def _sync(self, E, reads, writes):
        need = {}

        def req(t):
            if t is None:
                return
            k = id(t[0])
            if k not in need or need[k][1] < t[1]:
                need[k] = t

        for r in reads:
            req(r.w)
            if r.name.startswith("ps"):
                for t in r.r.values():
                    if t[0] is not E.sem:
                        req(t)
        for w in writes:
            req(w.w)
            for t in w.r.values():
                req(t)
        for k, (sem, val) in need.items():
            if sem is E.sem and E.self_ok:
                continue
            if E.known.get(k, 0) >= val:
                continue
            E.eng.wait_ge(sem, val)
            E.known[k] = val

    @staticmethod
    def _post(t, reads, writes):
        k = id(t[0])
        for r in reads:
            if k not in r.r or r.r[k][1] < t[1]:
                r.r[k] = t
        for w in writes:
            w.w = t
            w.r = {}

    def op(self, E, fn, reads=(), writes=(), signal=True):
        self._sync(E, reads, writes)
        ins = fn()
        if signal:
            ins.then_inc(E.sem, 1)
            E.cnt += 1
            t = (E.sem, E.cnt)
        else:
            t = (E.sem, E.cnt + 1)
        self._post(t, reads, writes)

    def dma(self, Q, out, in_, reads, writes, semres):
        self._sync(Q, reads, writes)
        if semres.dsem is None:
            semres.dsem = {}
            semres.dcnt = {}
        kind = "sw" if Q is self.pool else "hw"
        if kind not in semres.dsem:
            semres.dsem[kind] = self.newsem("d" + kind + semres.name)
            semres.dcnt[kind] = 0
            self.dsems.append((semres, kind))
        semres.dcnt[kind] += 1
        Q.eng.dma_start(out=out, in_=in_).then_inc(semres.dsem[kind], 16)
        t = (semres.dsem[kind], 16 * semres.dcnt[kind])
        self._post(t, reads, writes)

    def barrier(self):
        tickets = {}
        for E in self.engs:
            if E.cnt > 0:
                tickets[id(E.sem)] = (E.sem, E.cnt)
        for r, kind in self.dsems:
            tickets[id(r.dsem[kind])] = (r.dsem[kind], 16 * r.dcnt[kind])
        for E in self.engs:
            for k, (sem, val) in tickets.items():
                if sem is E.sem:
                    continue
                if E.known.get(k, 0) >= val:
                    continue
                E.eng.wait_ge(sem, val)
                E.known[k] = val

    def mm(self, out, lhsT, rhs, start, stop, reads, writes, signal=None):
        nc = self.nc
        if signal is None:
            signal = stop
        self.op(self.pe, lambda: nc.tensor.matmul(out, lhsT=lhsT, rhs=rhs, start=start, stop=stop),
                reads, writes, signal)

    def tr(self, out, in_, ident, reads, writes, signal=True):
        nc = self.nc
        self.op(self.pe, lambda: nc.tensor.transpose(out, in_, ident), reads, writes, signal)

    def actf(self, out, in_, func, reads, writes, bias=None, scale=None, accum_out=None):
        nc = self.nc
        kw = {}
        if bias is not None:
            kw["bias"] = bias
        if scale is not None:
            kw["scale"] = scale
        if accum_out is not None:
            kw["accum_out"] = accum_out
        self.op(self.act, lambda: nc.scalar.activation(out=out, in_=in_, func=func, **kw), reads, writes)

    def tt(self, out, in0, in1, op, reads, writes, E=None):
        E = E or self.dve
        self.op(E, lambda: E.eng.tensor_tensor(out=out, in0=in0, in1=in1, op=op), reads, writes)

    def ts(self, out, in0, s1, s2, op0, op1, reads, writes, E=None):
        E = E or self.dve
        if op1 is None:
            self.op(E, lambda: E.eng.tensor_scalar(out=out, in0=in0, scalar1=s1, scalar2=None, op0=op0),
                    reads, writes)
        else:
            self.op(E, lambda: E.eng.tensor_scalar(out=out, in0=in0, scalar1=s1, scalar2=s2, op0=op0, op1=op1),
                    reads, writes)

    def stt(self, out, in0, scalar, in1, op0, op1, reads, writes):
        nc = self.nc
        self.op(self.dve, lambda: nc.vector.scalar_tensor_tensor(out=out, in0=in0, scalar=scalar, in1=in1,
                                                                 op0=op0, op1=op1), reads, writes)

    def cp(self, out, in_, reads, writes, E=None):
        E = E or self.dve
        self.op(E, lambda: E.eng.tensor_copy(out=out, in_=in_), reads, writes)

    def memset(self, ap, val, writes, E=None):
        E = E or self.dve
        self.op(E, lambda: E.eng.memset(ap, val), (), writes)


class _Stop(Exception):
    pass


def build_program(SEQ=SEQ, stop=None):
    nc = bass.Bass("TRN2", target_bir_lowering=False)
    es = ExitStack()
    K = Kern(nc, es)

    def din(name, shape, dt=F32):
        return nc.dram_tensor(name, list(shape), dt, kind="ExternalInput")

    def dout(name, shape, dt=F32):
        return nc.dram_tensor(name, list(shape), dt, kind="ExternalOutput")

    def dscr(name, shape, dt=F32):
        return nc.dram_tensor(name, list(shape), dt, kind="Internal")

    xp = din("xp", [SEQ, D]).ap()
    xs_in = din("xs", [NS, D]).ap()
    cT_in = din("cT", [128, 8, 5]).ap()
    ck_in = din("cache_k", [NL, NSB, 128, 128]).ap()
    cv_in = din("cache_v", [NL, NSB, 128, 128]).ap()
    st_in = din("state", [NL, NSB, 128, 4, 128]).ap()
    gmix5_in = din("gmix5", [NL, 128, 8, 5]).ap()
    gmlp5_in = din("gmlp5", [NL, 128, 8, 5]).ap()
    gfin_in = din("gfin_bc", [128, D]).ap()
    wada_in = din("w_ada", [NL, D, 6 * D]).ap()
    bada5_in = din("bada5", [NL, 128, 48, 5]).ap()
    badarow_in = din("bada_row", [NL, 5, 2, D]).ap()
    win_in = din("w_in", [NL, D, 2816]).ap()
    wout_in = din("w_out", [NL, D, D]).ap()
    wup_in = din("w_up", [NL, D, DFF]).ap()
    wdn_in = din("w_down", [NL, DFF, D]).ap()
    winp_in = din("w_in_p", [NL, D, 2816]).ap()
    woutp_in = din("w_out_p", [NL, D, D]).ap()
    wupp_in = din("w_up_p", [NL, D, DFF]).ap()
    wdnp_in = din("w_down_p", [NL, DFF, D]).ap()
    sink_in = din("sinks", [NL, 128, 4]).ap()
    relb_in = din("rel_bias", [32, 8]).ap()
    cos_in = din("cosT", [128, SEQ]).ap()
    sin_in = din("sinT", [128, SEQ]).ap()
    coss_in = din("cosTs", [128, NS]).ap()
    sins_in = din("sinTs", [128, NS]).ap()
    cst_in = din("cst128", [128, 12, 128]).ap()
    oh_in = din("onehot", [32, 384]).ap()
    valid_in = din("valid", [128, 8, 256]).ap()
    rc128_in = din("retc128", [128, 4, 512]).ap()
    rc16_in = din("retc16", [128, 4, 512]).ap()

    y_p = dout("y_p", [SEQ, D]).ap()
    y_s = dout("y_s", [NS, D]).ap()
    wk_p = dout("wk_p", [NL, 128, 128]).ap()
    wv_p = dout("wv_p", [NL, 128, 128]).ap()
    st_p = dout("st_p", [NL, 128, 4, 128]).ap()
    wk_s = dout("wk_s", [NL, NSB, 128, 128]).ap()
    wv_s = dout("wv_s", [NL, NSB, 128, 128]).ap()
    st_s = dout("st_s", [NL, NSB, 128, 4, 128]).ap()

    xsc = [dscr("xsc0", [SEQ, D]).ap(), dscr("xsc1", [SEQ, D]).ap()]
    xssc = [dscr("xssc0", [NS, D]).ap(), dscr("xssc1", [NS, D]).ap()]
    gasc = dscr("gasc", [NL, 2, 192, D]).ap()
    usc_t = dscr("usc", [8, 384])
    esc = dscr("esc", [128, 8 * 256 + 8 * 16 + 8 * 16], BF16).ap()

    arena = K.sbuf("arena", [128, 65536], BF16)
    aux = K.sbuf("aux", [128, 8192], BF16)
    xt = [K.sbuf(f"xt{i}", [128, D], F32) for i in range(4)]
    xt_r = [K.res(f"xt{i}") for i in range(4)]
    xsb = K.sbuf("xsb", [128, 4, D], BF16)
    xsb_r = [K.res(f"xsb{i}") for i in range(4)]
    hT = K.sbuf("hT", [128, 8, TM], BF16)
    hT_r = K.res("hT")
    gabc = K.sbuf("gabc", [128, D], F32)
    gabc_r = K.res("gabc")
    gas = K.sbuf("gas", [NS, D], F32)
    gas_r = K.res("gas")
    gfin = K.sbuf("gfin", [128, D], F32)
    gfin_r = K.res("gfin")
    cst = K.sbuf("cst", [128, 12, 128], F32)
    cst_r = K.res("cst")
    cstb = K.sbuf("cstb", [128, 6, 128], BF16)
    cstb_r = K.res("cstb")
    mfm = K.sbuf("mfm", [128, 48, 5], F32)
    mfm_r = K.res("mfm")
    AB = K.sbuf("AB", [128, 2, 8, 5], F32)
    AB_r = K.res("AB")
    small = K.sbuf("small", [128, 64], F32)
    small_r = [K.res(f"small{i}") for i in range(16)]
    S = K.sbuf("S", [128, 4, 128], F32)
    S_r = K.res("S")
    f32a = K.sbuf("f32a", [128, 512], F32)
    f32a_r = K.res("f32a")
    f32b = K.sbuf("f32b", [128, 512], F32)
    f32b_r = K.res("f32b")
    f32c = K.sbuf("f32c", [128, 512], F32)
    f32c_r = K.res("f32c")
    esink = K.sbuf("esink", [128, 4], F32)
    esink_r = K.res("esink")
    scT = K.sbuf("scT", [128, 8, 5], BF16)
    scT_r = K.res("scT")
    cT = K.sbuf("cTsb", [128, 8, 5], F32)
    cT_r = K.res("cT")

    psf = [es.enter_context(nc.psum_tensor(f"psf{i}", [128, 512], F32)) for i in range(6)]
    psf_r = [K.res(f"psf{i}") for i in range(6)]
    psb = [es.enter_context(nc.psum_tensor(f"psb{i}", [128, 1024], BF16)) for i in range(2)]
    psb_r = [K.res(f"psb{i}") for i in range(2)]
    rot = {"f": 0, "b": 0}
    held = set()

    def nps():
        while True:
            i = rot["f"] % 6
            rot["f"] += 1
            if i not in held:
                return psf[i], psf_r[i]

    def npb():
        i = rot["b"] % 2
        rot["b"] += 1
        return psb[i], psb_r[i]

    def carve(base, off, nelem, dt, shape=None):
        ap = base[:, off:off + nelem]
        if dt == F32:
            ap = ap.bitcast(F32)
        return ap

    w_in_sb = arena[:, 0:8 * WIN_COLS].rearrange("p (k n) -> p k n", k=8)
    w_out_sb = arena[:, 23552:23552 + 8192].rearrange("p (k n) -> p k n", k=8)
    wM_r = K.res("wM")
    w_up_sb = arena[:, 0:32768].rearrange("p (k n) -> p k n", k=8)
    w_dn_sb = arena[:, 32768:65536].rearrange("p (k n) -> p k n", k=32)
    wF_r = K.res("wF")
    wD_r = K.res("wD")

    ov = {"off": 31744}

    def ovl(nelem, name):
        o = ov["off"]
        ov["off"] += nelem
        assert ov["off"] <= 65536, name
        return arena[:, o:o + nelem], K.res(name)

    qaT_f, qaT_r = ovl(4 * TM, "qaT")
    qaT = qaT_f.rearrange("p (m t) -> p m t", m=4)
    kT_f, kT_r = ovl(2 * 5 * 128, "kT")
    kT = kT_f.rearrange("p (v t) -> p v t", v=2)
    Vp_f, Vp_r = ovl(5 * 2 * 2 * 128, "Vpad")
    Vp = Vp_f.rearrange("p (s v h e) -> p s v h e", s=5, v=2, h=2)
    e_b = []
    for i in range(2):
        f, r = ovl(1024, f"e{i}")
        e_b.append((f.rearrange("p (h q) -> p h q", h=8), r))
    P_b = []
    for i in range(4):
        f, r = ovl(1024, f"P{i}")
        P_b.append((f.rearrange("p (h q) -> p h q", h=8), r))
    oT_f, oT_r = ovl(4 * TM, "oT")
    oT = oT_f.rearrange("p (m t) -> p m t", m=4)
    qrT_f, qrT_r = ovl(4 * TM, "qrT")
    qrT = qrT_f.rearrange("p (m t) -> p m t", m=4)
    krT_f, krT_r = ovl(4 * TM, "krT")
    krT = krT_f.rearrange("p (m t) -> p m t", m=4)
    grT_f, grT_r = ovl(4 * TM, "grT")
    grT = grT_f.rearrange("p (m t) -> p m t", m=4)
    vr_f, vr_r = ovl(4 * 512, "vr")
    vr = vr_f.rearrange("p (t e) -> p t e", t=4)
    mx_f, mx_r = ovl(4 * TM, "mixret")
    mx = mx_f.rearrange("p (m t) -> p m t", m=4)
    rt_b = [ovl(TM, f"rt{i}") for i in range(2)]
    kt_b = []
    for i in range(2):
        f, r = ovl(512, f"ktil{i}")
        kt_b.append((f.rearrange("p (h d) -> p h d", h=4), r))
    sc_b = []
    for i in range(2):
        f, r = ovl(512, f"scT{i}")
        sc_b.append((f.rearrange("p (h d) -> p h d", h=4), r))
    qt_f, qt_r = ovl(512, "qtil")
    qtl = qt_f.rearrange("p (h d) -> p h d", h=4)
    sq_f, sq_r = ovl(512, "sq")
    Sbf_b = []
    for i in range(2):
        f, r = ovl(512, f"Sbf{i}")
        Sbf_b.append((f.rearrange("p (h d) -> p h d", h=4), r))
    cs_b = []
    for i in range(2):
        f, r = ovl(2 * TM, f"cs{i}")
        cs_b.append((f.rearrange("p (c t) -> p c t", c=2), r))
    E_f, E_r = ovl(8 * 256, "E")
    E = E_f.rearrange("p (h q) -> p h q", h=8)
    Es0_f, Es0_r = ovl(8 * 16, "Es0")
    Es0 = Es0_f.rearrange("p (h q) -> p h q", h=8)
    Es1_f, Es1_r = ovl(8 * 16, "Es1")
    Es1 = Es1_f.rearrange("p (h q) -> p h q", h=8)

    rc128 = aux[:, 0:4096].bitcast(F32).rearrange("p (c n) -> p c n", c=4)
    rc16 = aux[:, 4096:8192].bitcast(F32).rearrange("p (c n) -> p c n", c=4)
    rcc_r = K.res("retc")
    uT = aux.rearrange("p (f t) -> p f t", f=32)
    uT_r = K.res("uT")
    aux_r = K.res("aux")

    ident = cstb[:, 0, :]
    Rt = cstb[:, 1, :]
    ones_bf = cstb[:, 2, :]
    opad = [cstb[:, 3, :], cstb[:, 4, :]]
    ident_f = cst[:, 0, :]
    J_f = cst[:, 2, :]

    SP, PL, ACT, DVE, PE = K.sp, K.pool, K.act, K.dve, K.pe

    def chk0(tag):
        if stop == tag:
            raise _Stop()

    try:
        K.dma(SP, cst[:], cst_in, (), [cst_r], cst_r)
        K.dma(SP, gfin[:], gfin_in, (), [gfin_r], gfin_r)
        K.dma(SP, cT[:], cT_in, (), [cT_r], cT_r)
        for i, j in enumerate([0, 1, 3, 4, 5]):
            K.cp(cstb[:, i, :], cst[:, j, :], [cst_r], [cstb_r])
        K.actf(scT[:], cT[:], AF.Silu, [cT_r], [scT_r])

        chk0("s1")
        relb = f32a[0:32, 0:8]
        ohs = f32b[0:32, 0:384]
        K.dma(SP, relb, relb_in, (), [f32a_r], f32a_r)
        K.dma(SP, ohs, oh_in, (), [f32b_r], f32b_r)
        pu, pu_r = nps()
        K.mm(pu[0:8, 0:384], relb, ohs, True, True, [f32a_r, f32b_r], [pu_r])
        usb = f32c[0:8, 0:384]
        K.actf(usb, pu[0:8, 0:384], AF.Copy, [pu_r], [f32c_r])
        usc_r = K.res("usc")
        K.dma(SP, usc_t.ap(), usb, [f32c_r], [usc_r], f32c_r)
        chk0("s2")
        valid_sb = xt[0][:, 0:2048].bitcast(BF16) if False else None
        erev = xt[1].rearrange("p (h q) -> p h q", h=8)[:, :, 0:128]
        vmask = xt[2].rearrange("p (h q) -> p h q", h=8)
        for half in range(2):
            src = bass.AP(usc_t, half * 128, [[1, 128], [384, 8], [1, 128]])
            K.dma(SP, erev, src, [usc_r], [xt_r[1]], xt_r[1])
            K.dma(SP, vmask, valid_in[:, :, half * 128:(half + 1) * 128], (), [xt_r[2]], xt_r[2])
            for hb in range(2):
                pb, pb_r = nps()
                K.mm(pb[:, :], J_f, xt[1][:, hb * 512:(hb + 1) * 512], True, True, [cst_r, xt_r[1]], [pb_r])
                vm = vmask[:, hb * 4:(hb + 1) * 4, :]
                t_neg = f32a[:, :].rearrange("p (h q) -> p h q", h=4)
                t_val = f32b[:, :].rearrange("p (h q) -> p h q", h=4)
                K.ts(t_neg, vm, 30000.0, -30000.0, ALU.mult, ALU.add, [xt_r[2]], [f32a_r])
                K.tt(t_val, pb[:, :].rearrange("p (h q) -> p h q", h=4), vm, ALU.mult, [pb_r, xt_r[2]], [f32b_r])
                K.tt(E[:, hb * 4:(hb + 1) * 4, half * 128:(half + 1) * 128], t_val, t_neg, ALU.add,
                     [f32a_r, f32b_r], [E_r])
        chk0("s3")
        er0 = xt[3][:, 0:128].rearrange("p (h q) -> p h q", h=8)
        src = bass.AP(usc_t, 128, [[1, 128], [384, 8], [1, 16]])
        K.dma(SP, er0, src, [usc_r], [xt_r[3]], xt_r[3])
        pb, pb_r = nps()
        K.mm(pb[:, 0:128], J_f, xt[3][:, 0:128], True, True, [cst_r, xt_r[3]], [pb_r])
        K.cp(Es0[:, :, :], pb[:, 0:128].rearrange("p (h q) -> p h q", h=8), [pb_r], [Es0_r])
        er1 = xt[0][0:16, 0:128].rearrange("p (h q) -> p h q", h=8)
        src = bass.AP(usc_t, 112, [[1, 16], [384, 8], [1, 16]])
        K.dma(SP, er1, src, [usc_r], [xt_r[0]], xt_r[0])
        pb, pb_r = nps()
        K.mm(pb[0:16, 0:128], cst[0:16, 2, 112:128], xt[0][0:16, 0:128], True, True, [cst_r, xt_r[0]], [pb_r])
        K.cp(Es1[0:16, :, :], pb[0:16, 0:128].rearrange("p (h q) -> p h q", h=8), [pb_r], [Es1_r])
        chk0("s4")
        esc_r = K.res("esc")
        K.dma(SP, esc[:, 0:2048], E_f, [E_r], [esc_r], E_r)
        K.dma(SP, esc[:, 2048:2176], Es0_f, [Es0_r], [esc_r], Es0_r)
        K.dma(SP, esc[0:16, 2176:2304], Es1_f[0:16, :], [Es1_r], [esc_r], Es1_r)


    except _Stop:
        pass
    def load_weights_M(l, prompt=True, consts=True):
        wv = (winp_in if prompt else win_in)[l].rearrange("(k p) n -> p k n", p=128)
        segs = [(0, 0, 512)]
        for kv in range(2):
            for dup in range(2):
                segs.append((512 + kv * 128 + dup * 64, 512 + kv * 64, 64))
        segs += [(768, 768, 512), (1280, 1280, 512), (1792, 2304, 512),
                 (2304, 640, 128), (2432, 1792, 512)]
        for dst, srcc, n in segs:
            K.dma(PL, w_in_sb[:, :, dst:dst + n], wv[:, :, srcc:srcc + n], (), [wM_r], wM_r)
        K.dma(PL, w_out_sb, (woutp_in if prompt else wout_in)[l].rearrange("(k p) n -> p k n", p=128), (), [wM_r],
              wM_r)
        if not consts:
            return
        K.dma(SP, E_f, esc[:, 0:2048], [esc_r], [E_r], E_r)
        K.dma(SP, Es0_f, esc[:, 2048:2176], [esc_r], [Es0_r], Es0_r)
        K.dma(SP, Es1_f[0:16, :], esc[0:16, 2176:2304], [esc_r], [Es1_r], Es1_r)
        K.dma(SP, esink[:], sink_in[l], (), [esink_r], esink_r)
        K.actf(esink[:], esink[:], AF.Exp, [esink_r], [esink_r])
        K.memset(Vp_f, 0.0, [Vp_r])
        K.memset(S[:], 0.0, [S_r])
        K.memset(Sbf_b[1][0], 0.0, [Sbf_b[1][1]])

    def load_aux_consts_M():
        K.dma(SP, rc128, rc128_in, (), [rcc_r], rcc_r)
        K.dma(SP, rc16, rc16_in, (), [rcc_r], rcc_r)

    def load_weights_F(l, prompt=True):
        wu = wupp_in if prompt else wup_in
        for kk in range(8):
            K.dma(PL, w_up_sb[:, kk, :], wu[l, kk * 128:(kk + 1) * 128, :], (), [wF_r], wF_r)
        wd = (wdnp_in if prompt else wdn_in)[l].rearrange("(k p) n -> p k n", p=128)
        for q in range(4):
            K.dma(PL, w_dn_sb[:, q * 8:(q + 1) * 8, :], wd[:, q * 8:(q + 1) * 8, :], (), [wD_r], wD_r)

    def ada(l):
        held.clear()
        pa, pa_r = nps()
        held.add(psf.index(pa))
        wa = [aux[:, 0:4096].rearrange("p (k n) -> p k n", k=8), aux[:, 4096:8192].rearrange("p (k n) -> p k n", k=8)]
        wa_r = [K.res("wa0"), K.res("wa1")]
        mrow = xt[3].rearrange("p (a n) -> p a n", a=1)
        mrow_t = [xt[2], xt[3]]
        mrow_r = [xt_r[2], xt_r[3]]
        brow = xt[1]
        K.dma(SP, brow[0:5, :], badarow_in[l, :, 0, :], (), [xt_r[1]], xt_r[1])
        brow2 = xt[0]
        K.dma(SP, brow2[0:5, :], badarow_in[l, :, 1, :], (), [xt_r[0]], xt_r[0])
        for j in range(12):
            w = wa[j % 2]
            wr = wa_r[j % 2]
            K.dma(PL, w, wada_in[l].rearrange("(k p) n -> p k n", p=128)[:, :, j * 512:(j + 1) * 512], (), [wr], wr)
            for cb in range(4):
                blk = j * 4 + cb
                for kk in range(8):
                    K.mm(pa[:, blk * 5:(blk + 1) * 5], w[:, kk, cb * 128:(cb + 1) * 128], scT[:, kk, :],
                         kk == 0, kk == 7, [wr, scT_r], [pa_r])
            if j in (4, 5, 10, 11):
                which = 0 if j < 6 else 1
                half = j % 2
                pr, pr_r = nps()
                for kk in range(8):
                    K.mm(pr[0:5, :], scT[:, kk, :], w[:, kk, :], kk == 0, kk == 7, [wr, scT_r], [pr_r])
                bsrc = (brow if which == 0 else brow2)
                bsr = (xt_r[1] if which == 0 else xt_r[0])
                K.tt(mrow_t[which][0:5, half * 512:(half + 1) * 512], pr[0:5, :],
                     bsrc[0:5, half * 512:(half + 1) * 512], ALU.add, [pr_r, bsr], [mrow_r[which]])
        b5 = f32a[:, 0:240].rearrange("p (a b) -> p a b", b=5)
        K.dma(SP, b5, bada5_in[l], (), [f32a_r], f32a_r)
        K.tt(mfm[:], pa[:, 0:240].rearrange("p (a b) -> p a b", b=5), b5, ALU.add, [pa_r, f32a_r], [mfm_r])
        held.clear()
        g5 = f32b[:, 0:40].rearrange("p (a b) -> p a b", b=5)
        K.dma(SP, g5, gmix5_in[l], (), [f32b_r], f32b_r)
        K.stt(AB[:, 0, :, :], mfm[:, 8:16, :], 1.0, g5, ALU.add, ALU.mult, [mfm_r, f32b_r], [AB_r])
        g5b = f32c[:, 0:40].rearrange("p (a b) -> p a b", b=5)
        K.dma(SP, g5b, gmlp5_in[l], (), [f32c_r], f32c_r)
        K.stt(AB[:, 1, :, :], mfm[:, 32:40, :], 1.0, g5b, ALU.add, ALU.mult, [mfm_r, f32c_r], [AB_r])
        for which in range(2):
            for half in range(2):
                pg, pg_r = nps()
                K.mm(pg[:, :], cst[0:5, 6, :], mrow_t[which][0:5, half * 512:(half + 1) * 512], True, True,
                     [cst_r, mrow_r[which]], [pg_r])
                K.cp(gabc[:, half * 512:(half + 1) * 512], pg[:, :], [pg_r], [gabc_r])
                pg2, pg2_r = nps()
                K.mm(pg2[0:NS, :], cst[0:5, 7, 0:NS], mrow_t[which][0:5, half * 512:(half + 1) * 512], True, True,
                     [cst_r, mrow_r[which]], [pg2_r])
                K.cp(gas[:, half * 512:(half + 1) * 512], pg2[0:NS, :], [pg2_r], [gas_r])
            K.dma(SP, gasc[l, which, 0:128, :], gabc[:], [gabc_r], [gasc_r[l][which]], gabc_r)
            K.dma(SP, gasc[l, which, 128:192, :], gas[:], [gas_r], [gasc_r[l][which]], gas_r)

    gasc_r = [[K.res(f"gasc{l}{w}") for w in range(2)] for l in range(NL)]

    def load_gates(l, which):
        K.dma(SP, gabc[:], gasc[l, which, 0:128, :], [gasc_r[l][which]], [gabc_r], gabc_r)
        K.dma(SP, gas[:], gasc[l, which, 128:192, :], [gasc_r[l][which]], [gas_r], gas_r)

    sm = {"i": 0}

    def norm_tile(xtile, xres, np_, slot):
        i = sm["i"] % 8
        sm["i"] += 1
        ss = small[0:np_, 2 * i:2 * i + 1]
        rr = small[0:np_, 2 * i + 1:2 * i + 2]
        sr = small_r[i]
        junk = xsb[0:np_, slot, :]
        K.actf(junk, xtile[0:np_, :], AF.Square, [xres], [xsb_r[slot], sr], accum_out=ss)
        K.actf(rr, ss, AF.Ln, [sr], [sr], bias=float(EPS), scale=1.0 / D)
        K.actf(rr, rr, AF.Exp, [sr], [sr], scale=-0.5)
        K.ts(xsb[0:np_, slot, :], xtile[0:np_, :], rr, None, ALU.mult, None, [xres, sr], [xsb_r[slot]])

    def transpose_mod(slots, np_, which, bsel):
        nt = len(slots) * np_
        for kp in range(4):
            pb, pb_r = npb()
            for kk in range(2):
                k = kp * 2 + kk
                for ti, sl in enumerate(slots):
                    K.tr(pb[:, kk * 512 + ti * np_: kk * 512 + (ti + 1) * np_], xsb[0:np_, sl, k * 128:(k + 1) * 128],
                         ident[0:np_, 0:np_], [xsb_r[sl], cstb_r], [pb_r])
            for kk in range(2):
                k = kp * 2 + kk
                for (c0, c1, b) in bsel(nt):
                    K.actf(hT[:, k, c0:c1], pb[:, kk * 512 + c0: kk * 512 + c1], AF.Identity, [pb_r, AB_r, mfm_r],
                           [hT_r], bias=mfm[:, (0 if which == 0 else 24) + k, b:b + 1], scale=AB[:, which, k, b:b + 1])

    rti = {"i": 0}

    def rot_start(pp, pp_r, dst, dst_r, cs, cs_r, nt):
        xb, xb_r = rt_b[rti["i"] % 2]
        rti["i"] += 1
        K.actf(xb[:, 0:nt], pp[:, 0:nt], AF.Copy, [pp_r], [xb_r])
        return (xb, xb_r, dst, dst_r, cs, cs_r, nt)

    def rot_finish(pend):
        (xb, xb_r, dst, dst_r, cs, cs_r, nt) = pend
        p2, p2_r = nps()
        K.mm(p2[:, 0:nt], Rt, xb[:, 0:nt], True, True, [cstb_r, xb_r], [p2_r])
        K.tt(f32b[:, 0:nt], xb[:, 0:nt], cs[:, 0, 0:nt], ALU.mult, [xb_r, cs_r], [f32b_r])
        K.tt(f32a[:, 0:nt], p2[:, 0:nt], cs[:, 1, 0:nt], ALU.mult, [p2_r, cs_r], [f32a_r])
        K.tt(dst, f32b[:, 0:nt], f32a[:, 0:nt], ALU.add, [f32b_r, f32a_r], [dst_r])

    def proj_block(blk, nt, cs, cs_r):
        pp, pp_r = nps()
        for kk in range(8):
            K.mm(pp[:, 0:nt], w_in_sb[:, kk, blk * 128:(blk + 1) * 128], hT[:, kk, 0:nt], kk == 0, kk == 7,
                 [wM_r, hT_r], [pp_r])
        if blk < 4:
            K.actf(qaT[:, blk, 0:nt], pp[:, 0:nt], AF.Copy, [pp_r], [qaT_r], scale=0.125)
        elif blk < 6:
            K.cp(kT[:, blk - 4, 128:128 + nt], pp[:, 0:nt], [pp_r], [kT_r])
        elif blk < 10:
            return rot_start(pp, pp_r, qrT[:, blk - 6, 0:nt], qrT_r, cs, cs_r, nt)
        elif blk < 14:
            return rot_start(pp, pp_r, krT[:, blk - 10, 0:nt], krT_r, cs, cs_r, nt)
        else:
            K.actf(grT[:, blk - 14, 0:nt], pp[:, 0:nt], AF.Silu, [pp_r], [grT_r])
        return None

    def proj_blocks(order, nt, cs, cs_r):
        pend = None
        for blk in order:
            newp = proj_block(blk, nt, cs, cs_r)
            if pend is not None:
                rot_finish(pend)
            pend = newp
        if pend is not None:
            rot_finish(pend)

    def project_fm(nt, cs, cs_r):
        proj_blocks(list(range(18)), nt, cs, cs_r)

    pbi = {"i": 0}

    def att_scores(q0, nq, blocks):
        Ps = []
        for (nk, kT_fn, V_fn, E_fn, rds) in blocks:
            Pt, P_r = P_b[pbi["i"] % 4]
            pbi["i"] += 1
            for hb in range(2):
                pp, pp_r = nps()
                hp = hb
                for hh in range(4):
                    m, kv = hh, hh // 2
                    K.mm(pp[0:nk, hh * nq:(hh + 1) * nq], kT_fn(hp, kv), qaT[hp * 64:(hp + 1) * 64, m, q0:q0 + nq],
                         hh == 0, False, rds + [qaT_r], [pp_r], signal=False)
                K.mm(pp[0:nk, 0:4 * nq].rearrange("p (h q) -> p h q", h=4), ident[0:nk, 0:nk], E_fn(hb), False, True,
                     [cstb_r, E_r, Es0_r, Es1_r], [pp_r], signal=True)
                K.actf(Pt[0:nk, hb * 4:(hb + 1) * 4, 0:nq], pp[0:nk, 0:4 * nq].rearrange("p (h q) -> p h q", h=4),
                       AF.Exp, [pp_r], [P_r])
            Ps.append((Pt, P_r))
        return (q0, nq, blocks, Ps)

    def att_pv(st):
        q0, nq, blocks, Ps = st
        po, po_r = nps()
        pd, pd_r = nps()
        nb = len(blocks)
        for (pt_, lfn, pr_) in ((po, 0, po_r), (pd, 1, pd_r)):
            for m in range(4):
                kv = m // 2
                cnt = 0
                for bi, (nk, kT_fn, V_fn, E_fn, rds) in enumerate(blocks):
                    for hp in range(2):
                        lhs = V_fn(kv, hp) if lfn == 0 else opad[hp][0:nk, :]
                        K.mm(pt_[:, m * nq:(m + 1) * nq], lhs, Ps[bi][0][0:nk, hp * 4 + m, 0:nq], cnt == 0,
                             cnt == 2 * nb - 1, rds + [Ps[bi][1], cstb_r], [pr_],
                             signal=(m == 3 and cnt == 2 * nb - 1))
                        cnt += 1
        for m in range(4):
            K.actf(f32b[:, m * nq:(m + 1) * nq], pd[:, m * nq:(m + 1) * nq], AF.Ln, [pd_r, esink_r], [f32b_r],
                   bias=esink[:, m:m + 1])
        K.actf(f32b[:, 0:4 * nq], f32b[:, 0:4 * nq], AF.Exp, [f32b_r], [f32b_r], scale=-1.0)
        K.tt(oT[:, :, q0:q0 + nq], po[:, 0:4 * nq].rearrange("p (m q) -> p m q", m=4),
             f32b[:, 0:4 * nq].rearrange("p (m q) -> p m q", m=4), ALU.mult, [po_r, f32b_r], [oT_r])

    rci = {"i": 0}

    def ret_front(c0, n, vr_fn, vr_reads, rc, Sb_in, Sb_out):
        maskT = rc[0:n, 0, 0:4 * n].rearrange("p (h q) -> p h q", h=4)
        qdec = rc[:, 1, 0:4 * n].rearrange("p (h q) -> p h q", h=4)
        kdec = rc[0:n, 2, :].rearrange("p (h q) -> p h q", h=4)
        i = rci["i"] % 2
        rci["i"] += 1
        ktl, ktl_r = kt_b[i]
        sct, sct_r = sc_b[i]
        pb, pb_r = npb()
        for h in range(4):
            K.tr(pb[0:n, h * 128:(h + 1) * 128], krT[:, h, c0:c0 + n], ident, [krT_r, cstb_r], [pb_r],
                 signal=(h == 3))
        K.tt(ktl[0:n, :, :], pb[0:n, 0:512].rearrange("p (h q) -> p h q", h=4), kdec, ALU.mult, [pb_r, rcc_r],
             [ktl_r])
        ps_, ps_r = nps()
        for h in range(4):
            K.mm(ps_[0:n, h * n:(h + 1) * n], krT[:, h, c0:c0 + n], qrT[:, h, c0:c0 + n], True, True,
                 [krT_r, qrT_r], [ps_r], signal=(h == 3))
        K.tt(sct[0:n, :, 0:n], ps_[0:n, 0:4 * n].rearrange("p (h q) -> p h q", h=4), maskT, ALU.mult,
             [ps_r, rcc_r], [sct_r])
        return dict(c0=c0, n=n, vr_fn=vr_fn, vr_reads=vr_reads, rc=rc, Sb_in=Sb_in, Sb_out=Sb_out, ktl=ktl,
                    ktl_r=ktl_r, sct=sct, sct_r=sct_r, qdec=qdec)

    def ret_pre(st):
        c0, n = st["c0"], st["n"]
        K.tt(qtl[:, :, 0:n], qrT[:, :, c0:c0 + n], st["qdec"], ALU.mult, [qrT_r, rcc_r], [qt_r])

    def ret_mid(st):
        c0, n, vr_fn, vr_reads, rc = st["c0"], st["n"], st["vr_fn"], st["vr_reads"], st["rc"]
        Sb_in, Sb_out, ktl, ktl_r, sct, sct_r = st["Sb_in"], st["Sb_out"], st["ktl"], st["ktl_r"], st["sct"], st["sct_r"]
        Gc = rc[:, 3, :].rearrange("p (h q) -> p h q", h=4)
        pdl, pdl_r = nps()
        for h in range(4):
            K.mm(pdl[:, h * 128:(h + 1) * 128], ktl[0:n, h, :], vr_fn(h), True, True, vr_reads + [ktl_r], [pdl_r],
                 signal=(h == 3))
        po, po_r = nps()
        for h in range(4):
            K.mm(po[:, h * n:(h + 1) * n], vr_fn(h), sct[0:n, h, 0:n], True, False, vr_reads + [sct_r], [po_r],
                 signal=False)
            K.mm(po[:, h * n:(h + 1) * n], Sb_in[0][:, h, :], qtl[:, h, 0:n], False, True, [Sb_in[1], qt_r], [po_r],
                 signal=(h == 3))
        K.actf(sq_f[:, 0:4 * n], po[:, 0:4 * n], AF.Square, [po_r], [sq_r])
        K.actf(f32c[:, 0:4 * n], po[:, 0:4 * n], AF.Copy, [po_r], [f32c_r])
        K.tt(S[:], S[:], Gc, ALU.mult, [S_r, rcc_r], [S_r])
        K.tt(S[:], S[:], pdl[:, :].rearrange("p (h q) -> p h q", h=4), ALU.add, [S_r, pdl_r], [S_r])
        K.actf(Sb_out[0], S[:], AF.Copy, [S_r], [Sb_out[1]])

    def ret_back(st):
        c0, n = st["c0"], st["n"]
        pn, pn_r = nps()
        K.mm(pn[:, 0:4 * n], ones_bf, sq_f[:, 0:4 * n], True, True, [cstb_r, sq_r], [pn_r])
        K.actf(f32a[:, 0:4 * n], pn[:, 0:4 * n], AF.Ln, [pn_r], [f32a_r], bias=float(EPS), scale=1.0 / 128.0)
        K.actf(f32a[:, 0:4 * n], f32a[:, 0:4 * n], AF.Exp, [f32a_r], [f32a_r], scale=-0.5)
        K.tt(f32a[:, 0:4 * n], f32c[:, 0:4 * n], f32a[:, 0:4 * n], ALU.mult, [f32c_r, f32a_r], [f32a_r])
        K.tt(mx[:, :, c0:c0 + n], f32a[:, 0:4 * n].rearrange("p (h q) -> p h q", h=4), grT[:, :, c0:c0 + n],
             ALU.mult, [f32a_r, grT_r], [mx_r])

    def attention(q0, nq, blocks):
        att_pv(att_scores(q0, nq, blocks))

    def retention(c0, n, vr_fn, vr_reads, rc, Sb_in, Sb_out):
        st = ret_front(c0, n, vr_fn, vr_reads, rc, Sb_in, Sb_out)
        ret_pre(st)
        ret_mid(st)
        ret_back(st)

    def fold_gate_into_w_out():
        for kk in range(8):
            K.tt(w_out_sb[:, kk, :], w_out_sb[:, kk, :], gabc[:, :], ALU.mult, [wM_r, gabc_r], [wM_r])

    def out_proj(t0, np_, xres_tile, xres_r, gate, gate_r, folded=False):
        for half in range(2):
            pp, pp_r = nps()
            for m in range(4):
                K.mm(pp[0:np_, :], oT[:, m, t0:t0 + np_], w_out_sb[:, m, half * 512:(half + 1) * 512], m == 0, False,
                     [oT_r, wM_r], [pp_r], signal=False)
            for h in range(4):
                K.mm(pp[0:np_, :], mx[:, h, t0:t0 + np_], w_out_sb[:, 4 + h, half * 512:(half + 1) * 512], False,
                     h == 3, [mx_r, wM_r], [pp_r])
            if folded:
                K.tt(xres_tile[0:np_, half * 512:(half + 1) * 512], pp[0:np_, :],
                     xres_tile[0:np_, half * 512:(half + 1) * 512], ALU.add, [pp_r, xres_r], [xres_r])
                continue
            K.tt(f32a[0:np_, :], pp[0:np_, :], gate[0:np_, half * 512:(half + 1) * 512], ALU.mult, [pp_r, gate_r],
                 [f32a_r])
            K.tt(xres_tile[0:np_, half * 512:(half + 1) * 512], xres_tile[0:np_, half * 512:(half + 1) * 512],
                 f32a[0:np_, :], ALU.add, [xres_r, f32a_r], [xres_r])

    def bsel_prompt(nt):
        return [(0, nt, 0)]

    def bsel_sample(nt):
        return [(b * 16, (b + 1) * 16, 1 + b) for b in range(NSB)]

    def phase_M_prompt(l, xin, xout, xout_r):
        NG = SEQ // TM
        xin_r = xres_store[id(xin)] if id(xin) in xres_store else None

        def rd(g):
            return [xin_r[g]] if xin_r is not None else []

        def nload(g, ts_):
            for t in ts_:
                row = g * TM + t * 128
                sl = t % 2
                K.dma(SP, xt[sl][:], xin[row:row + 128, :], rd(row // 128), [xt_r[sl]], xt_r[sl])

        def ncomp(g, ts_):
            for t in ts_:
                norm_tile(xt[t % 2], xt_r[t % 2], 128, t)

        nload(0, [0, 1])
        ncomp(0, [0, 1])
        nload(0, [2, 3])
        ncomp(0, [2, 3])
        if NG > 1:
            nload(1, [0, 1])
        chk("m0")
        transpose_mod([0, 1, 2, 3], 128, 0, bsel_prompt)
        chk("m1")
        def load_cs(g):
            cs, cs_r = cs_b[g % 2]
            K.dma(PL, cs[:, 0, :], cos_in[:, g * TM:(g + 1) * TM], (), [cs_r], cs_r)
            K.dma(PL, cs[:, 1, :], sin_in[:, g * TM:(g + 1) * TM], (), [cs_r], cs_r)

        load_cs(0)
        project_fm(TM, *cs_b[0])
        for g in range(NG):
            chk("m2")
            for t in range(4):
                pv, pv_r = nps()
                for kk in range(8):
                    K.mm(pv[:, 0:128], hT[:, kk, t * 128:(t + 1) * 128], w_in_sb[:, kk, 2304:2432], kk == 0, kk == 7,
                         [hT_r, wM_r], [pv_r])
                for hp in range(2):
                    K.cp(Vp[:, 1 + t, :, hp, hp * 64:(hp + 1) * 64],
                         pv[:, 0:128].rearrange("p (v e) -> p v e", v=2), [pv_r], [Vp_r])
                if g == NG - 1 and t == 3:
                    K.cp(f32c[:, 0:128], pv[:, 0:128], [pv_r], [f32c_r])
                    K.dma(SP, wv_p[l], f32c[:, 0:128], [f32c_r], [out_r], f32c_r)
                    pk, pk_r = nps()
                    for kk in range(8):
                        K.mm(pk[:, 0:256], hT[:, kk, 384:512], w_in_sb[:, kk, 512:768], kk == 0, kk == 7,
                             [hT_r, wM_r], [pk_r])
                    K.cp(f32b[:, 0:128].rearrange("p (v e) -> p v e", v=2),
                         pk[:, 0:256].rearrange("p (v e) -> p v e", v=2)[:, :, 0:64], [pk_r], [f32b_r])
                    K.dma(SP, wk_p[l], f32b[:, 0:128], [f32b_r], [out_r], f32b_r)
                pw, pw_r = nps()
                for kk in range(8):
                    K.mm(pw[:, :], hT[:, kk, t * 128:(t + 1) * 128], w_in_sb[:, kk, 2432:2944], kk == 0, kk == 7,
                         [hT_r, wM_r], [pw_r])
                K.cp(vr[:, t, :], pw[:, :], [pw_r], [vr_r])
            chk("m3")
            if g + 1 < NG:
                ncomp(g + 1, [0, 1])
                nload(g + 1, [2, 3])
            ast = [None] * 4
            rst = [None] * 4

            def alpha(t, g=g):
                blocks = []
                if not (g == 0 and t == 0):
                    blocks.append((128,
                                   (lambda hp, kv, t=t: kT[hp * 64:(hp + 1) * 64, kv, t * 128:(t + 1) * 128]),
                                   (lambda kv, hp, t=t: Vp[:, t, kv, hp, :]),
                                   (lambda hb: E[:, hb * 4:(hb + 1) * 4, 128:256]), [kT_r, Vp_r]))
                blocks.append((128,
                               (lambda hp, kv, t=t: kT[hp * 64:(hp + 1) * 64, kv, (t + 1) * 128:(t + 2) * 128]),
                               (lambda kv, hp, t=t: Vp[:, t + 1, kv, hp, :]),
                               (lambda hb: E[:, hb * 4:(hb + 1) * 4, 0:128]), [kT_r, Vp_r]))
                ast[t] = att_scores(t * 128, 128, blocks)
                ci = (g * 4 + t) % 2
                rst[t] = ret_front(t * 128, 128, (lambda h, t=t: vr[:, t, h * 128:(h + 1) * 128]), [vr_r], rc128,
                                   Sbf_b[1 - ci], Sbf_b[ci])

            def beta(t):
                ret_pre(rst[t])
                att_pv(ast[t])
                ret_mid(rst[t])

            def gamma1(t):
                ret_back(rst[t])

            def gamma2(t, g=g):
                row = g * TM + t * 128
                sl = 2 + (t % 2)
                K.dma(SP, xt[sl][:], xin[row:row + 128, :], rd(row // 128), [xt_r[sl]], xt_r[sl])
                out_proj(t * 128, 128, xt[sl], xt_r[sl], gabc, gabc_r, folded=True)
                K.dma(PL, xout[row:row + 128, :], xt[sl][:], [xt_r[sl]], [xout_r[row // 128]], xt_r[sl])

            for i in range(7):
                if 0 <= i - 2 < 4:
                    gamma1(i - 2)
                if 0 <= i - 1 < 4:
                    beta(i - 1)
                if i < 4:
                    alpha(i)
                if 0 <= i - 3 < 4:
                    gamma2(i - 3)
                if i == 2 and g + 1 < NG:
                    ncomp(g + 1, [2, 3])
                    if g + 2 < NG:
                        nload(g + 2, [0, 1])
                if i == 4:
                    K.cp(kT[:, :, 0:128], kT[:, :, 512:640], [kT_r], [kT_r])
                    K.cp(Vp[:, 0, :, :, :], Vp[:, 4, :, :, :], [Vp_r], [Vp_r])
                    if g + 1 < NG:
                        load_cs(g + 1)
                        transpose_mod([0, 1, 2, 3], 128, 0, bsel_prompt)
                if g + 1 < NG:
                    ncs = cs_b[(g + 1) % 2]
                    if i == 5:
                        proj_blocks([0, 1, 2, 3, 10, 11, 12, 13], TM, *ncs)
                    if i == 6:
                        proj_blocks([6, 7, 8, 9, 4, 5, 14, 15, 16, 17], TM, *ncs)
                chk(f"L{i}")
            chk("m5")
        K.dma(SP, st_p[l], S[:], [S_r], [out_r], S_r)

    def mlp_group(ntiles, np_, which_gate, gate, gate_r, xtiles, bsel, final, store_fn):
        nt = ntiles * np_
        for f in range(32):
            pp, pp_r = nps()
            for kk in range(8):
                K.mm(pp[:, 0:nt], w_up_sb[:, kk, f * 128:(f + 1) * 128], hT[:, kk, 0:nt], kk == 0, kk == 7,
                     [wF_r, hT_r], [pp_r])
            fb, fb_r = (f32a, f32a_r) if f % 2 == 0 else (f32b, f32b_r)
            K.actf(fb[:, 0:nt], pp[:, 0:nt], AF.Relu, [pp_r], [fb_r])
            K.tt(uT[:, f, 0:nt], fb[:, 0:nt], fb[:, 0:nt], ALU.mult, [fb_r], [uT_r])

    def mlp_down(ti, np_, xtile, xtile_r, gate, gate_r, final, store_fn, split=False):
        for half in range(2):
            pp, pp_r = nps()
            for f in range(32):
                K.mm(pp[0:np_, :], uT[:, f, ti * np_:(ti + 1) * np_], w_dn_sb[:, f, half * 512:(half + 1) * 512],
                     f == 0, f == 31, [uT_r, wD_r], [pp_r])
            K.tt(f32c[0:np_, :], pp[0:np_, :], gate[0:np_, half * 512:(half + 1) * 512], ALU.mult, [pp_r, gate_r],
                 [f32c_r])
            K.tt(xtile[0:np_, half * 512:(half + 1) * 512], xtile[0:np_, half * 512:(half + 1) * 512],
                 f32c[0:np_, :], ALU.add, [xtile_r, f32c_r], [xtile_r])
        def fin():
            if final:
                i = sm["i"] % 8
                sm["i"] += 1
                ss = small[0:np_, 2 * i:2 * i + 1]
                rr = small[0:np_, 2 * i + 1:2 * i + 2]
                sr = small_r[i]
                K.actf(f32a[0:np_, :], xtile[0:np_, 0:512], AF.Square, [xtile_r], [f32a_r, sr], accum_out=ss)
                ss2 = small[0:np_, 32 + i:33 + i]
                K.actf(f32a[0:np_, :], xtile[0:np_, 512:1024], AF.Square, [xtile_r], [f32a_r, sr], accum_out=ss2)
                K.tt(ss, ss, ss2, ALU.add, [sr], [sr])
                K.actf(rr, ss, AF.Ln, [sr], [sr], bias=float(EPS), scale=1.0 / D)
                K.actf(rr, rr, AF.Exp, [sr], [sr], scale=-0.5)
                K.stt(xtile[0:np_, :], xtile[0:np_, :], rr, gfin[0:np_, :], ALU.mult, ALU.mult,
                      [xtile_r, sr, gfin_r], [xtile_r])
            store_fn()
        if split:
            return fin
        fin()
        return None

    def phase_F_prompt(l, xin, xin_r, xout, xout_r, final):
        NG = SEQ // TF

        def load_group(g):
            for t in range(2):
                row = g * TF + t * 128
                sl = (g % 2) * 2 + t
                K.dma(SP, xt[sl][:], xin[row:row + 128, :], [xin_r[row // 128]], [xt_r[sl]], xt_r[sl])

        def norm_group(g):
            for t in range(2):
                sl = (g % 2) * 2 + t
                norm_tile(xt[sl], xt_r[sl], 128, sl)

        load_group(0)
        norm_group(0)
        transpose_mod([0, 1], 128, 1, bsel_prompt)
        for g in range(NG):
            if g + 1 < NG:
                load_group(g + 1)
            mlp_group(2, 128, 1, gabc, gabc_r, None, bsel_prompt, final, None)
            if g + 1 < NG:
                norm_group(g + 1)
            for t in range(2):
                row = g * TF + t * 128
                sl = (g % 2) * 2 + t

                def store(row=row, sl=sl):
                    if final:
                        K.dma(PL, y_p[row:row + 128, :], xt[sl][:], [xt_r[sl]], [out_r], xt_r[sl])
                    else:
                        K.dma(PL, xout[row:row + 128, :], xt[sl][:], [xt_r[sl]], [xout_r[row // 128]], xt_r[sl])
                fin0 = mlp_down(t, 128, xt[sl], xt_r[sl], gabc, gabc_r, final, store, split=True)
                if t == 0 and g + 1 < NG:
                    sl0 = ((g + 1) % 2) * 2
                    transpose_mod([sl0, sl0 + 1], 128, 1, bsel_prompt)
                fin0()

    def phase_M_sample(l, xin, xin_r, xout, xout_r):
        K.dma(SP, xt[0][0:NS, :], xin, xin_r, [xt_r[0]], xt_r[0])
        norm_tile(xt[0], xt_r[0], NS, 0)
        transpose_mod([0], NS, 0, bsel_sample)
        cs, cs_r = cs_b[0]
        K.dma(PL, cs[:, 0, 0:NS], coss_in, (), [cs_r], cs_r)
        K.dma(PL, cs[:, 1, 0:NS], sins_in, (), [cs_r], cs_r)
        project_fm(NS, cs, cs_r)
        for b in range(NSB):
            c0 = b * 16
            pv, pv_r = nps()
            for kk in range(8):
                K.mm(pv[0:16, 0:128], hT[:, kk, c0:c0 + 16], w_in_sb[:, kk, 2304:2432], kk == 0, kk == 7,
                     [hT_r, wM_r], [pv_r])
            K.memset(Vp[:, 1, :, :, :], 0.0, [Vp_r])
            for hp in range(2):
                K.cp(Vp[0:16, 1, :, hp, hp * 64:(hp + 1) * 64], pv[0:16, 0:128].rearrange("p (v e) -> p v e", v=2),
                     [pv_r], [Vp_r])
            K.cp(f32c[0:16, 0:128], pv[0:16, 0:128], [pv_r], [f32c_r])
            K.dma(SP, wv_s[l, b, 112:128, :], f32c[0:16, 0:128], [f32c_r], [out_r], f32c_r)
            pk, pk_r = nps()
            for kk in range(8):
                K.mm(pk[0:16, 0:256], hT[:, kk, c0:c0 + 16], w_in_sb[:, kk, 512:768], kk == 0, kk == 7,
                     [hT_r, wM_r], [pk_r])
            K.cp(f32b[0:16, 0:128].rearrange("p (v e) -> p v e", v=2),
                 pk[0:16, 0:256].rearrange("p (v e) -> p v e", v=2)[:, :, 0:64], [pk_r], [f32b_r])
            K.dma(SP, wk_s[l, b, 112:128, :], f32b[0:16, 0:128], [f32b_r], [out_r], f32b_r)
            pw, pw_r = nps()
            for kk in range(8):
                K.mm(pw[0:16, :], hT[:, kk, c0:c0 + 16], w_in_sb[:, kk, 2432:2944], kk == 0, kk == 7,
                     [hT_r, wM_r], [pw_r])
            K.cp(vr[0:16, 0, :], pw[0:16, :], [pw_r], [vr_r])
            K.dma(SP, wk_s[l, b, 0:112, :], ck_in[l, b, 16:128, :], (), [out_r], out_r)
            K.dma(SP, wv_s[l, b, 0:112, :], cv_in[l, b, 16:128, :], (), [out_r], out_r)
            K.dma(SP, xt[2][:, 0:128], ck_in[l, b], (), [xt_r[2]], xt_r[2])
            K.dma(SP, xt[2][:, 128:256], cv_in[l, b], (), [xt_r[2]], xt_r[2])
            ckd, ckd_r = rt_b[0]
            for dup in range(2):
                K.cp(ckd[:, 0:256].rearrange("p (v d e) -> p v d e", v=2, d=2)[:, :, dup, :],
                     xt[2][:, 0:128].rearrange("p (v e) -> p v e", v=2), [xt_r[2]], [ckd_r])
            pb, pb_r = npb()
            for kv in range(2):
                K.tr(pb[:, kv * 128:(kv + 1) * 128], ckd[:, kv * 128:(kv + 1) * 128], ident, [ckd_r, cstb_r], [pb_r],
                     signal=(kv == 1))
            K.cp(kT[:, :, 0:128], pb[:, 0:256].rearrange("p (v t) -> p v t", v=2), [pb_r], [kT_r])
            for hp in range(2):
                K.cp(Vp[:, 0, :, hp, hp * 64:(hp + 1) * 64], xt[2][:, 128:256].rearrange("p (v e) -> p v e", v=2),
                     [xt_r[2]], [Vp_r])
            blocks = [
                (128, (lambda hp, kv: kT[hp * 64:(hp + 1) * 64, kv, 0:128]),
                 (lambda kv, hp: Vp[:, 0, kv, hp, :]),
                 (lambda hb: Es0[:, hb * 4:(hb + 1) * 4, :]), [kT_r, Vp_r]),
                (16, (lambda hp, kv, c0=c0: kT[hp * 64:(hp + 1) * 64, kv, 128 + c0:128 + c0 + 16]),
                 (lambda kv, hp: Vp[0:16, 1, kv, hp, :]),
                 (lambda hb: Es1[0:16, hb * 4:(hb + 1) * 4, :]), [kT_r, Vp_r]),
            ]
            attention(c0, 16, blocks)
            K.dma(SP, S[:], st_in[l, b], (), [S_r], S_r)
            K.actf(Sbf_b[0][0], S[:], AF.Copy, [S_r], [Sbf_b[0][1]])
            retention(c0, 16, (lambda h: vr[0:16, 0, h * 128:(h + 1) * 128]), [vr_r], rc16, Sbf_b[0], Sbf_b[1])
            K.dma(SP, st_s[l, b], S[:], [S_r], [out_r], S_r)
        K.dma(SP, xt[3][0:NS, :], xin, xin_r, [xt_r[3]], xt_r[3])
        out_proj(0, NS, xt[3], xt_r[3], gas, gas_r)
        K.dma(PL, xout, xt[3][0:NS, :], [xt_r[3]], [xout_r], xt_r[3])

    def phase_F_sample(l, xin, xin_r, xout, xout_r, final):
        K.dma(SP, xt[0][0:NS, :], xin, [xin_r], [xt_r[0]], xt_r[0])
        norm_tile(xt[0], xt_r[0], NS, 0)
        transpose_mod([0], NS, 1, bsel_sample)
        mlp_group(1, NS, 1, gas, gas_r, None, bsel_sample, final, None)

        def store():
            if final:
                K.dma(PL, y_s, xt[0][0:NS, :], [xt_r[0]], [out_r], xt_r[0])
            else:
                K.dma(PL, xout, xt[0][0:NS, :], [xt_r[0]], [xout_r], xt_r[0])
        mlp_down(0, NS, xt[0], xt_r[0], gas, gas_r, final, store)

    out_r = K.res("outputs")
    xres_store = {}
    xsc_r = [[K.res(f"xsc{i}_{j}") for j in range(SEQ // 128)] for i in range(2)]
    xssc_r = [K.res("xssc0"), K.res("xssc1")]
    xres_store[id(xsc[0])] = xsc_r[0]
    xres_store[id(xsc[1])] = xsc_r[1]

    def chk(tag):
        if stop == tag:
            raise _Stop()

    K.barrier()
    try:
        if stop in ("s1", "s2", "s3", "s4"):
            raise _Stop()
        chk("setup")
        for l in range(NL):
            load_weights_M(l)
            ada(l)
            K.barrier()
            chk(f"ada{l}")
            load_aux_consts_M()
            load_gates(l, 0)
            fold_gate_into_w_out()
            chk(f"wM{l}")
            xin_p = xp if l == 0 else xsc[1]
            phase_M_prompt(l, xin_p, xsc[0], xsc_r[0])
            chk(f"Mp{l}")
            K.dma(PL, w_out_sb, wout_in[l].rearrange("(k p) n -> p k n", p=128), (), [wM_r], wM_r)
            xin_s = xs_in if l == 0 else xssc[1]
            phase_M_sample(l, xin_s, [] if l == 0 else [xssc_r[1]], xssc[0], xssc_r[0])
            K.barrier()
            chk(f"Ms{l}")
            load_weights_F(l)
            load_gates(l, 1)
            final = (l == NL - 1)
            phase_F_prompt(l, xsc[0], xsc_r[0], xsc[1], xsc_r[1], final)
            chk(f"Fp{l}")
            phase_F_sample(l, xssc[0], xssc_r[0], xssc[1], xssc_r[1], final)
            K.barrier()
            chk(f"Fs{l}")
    except _Stop:
        pass
    K.barrier()
    es.close()
    return nc


_PROG = None
_STOP = None


def _consts(SEQ=SEQ):
    lg = np.log(1.0 - 2.0 ** (-5.0 - np.arange(4, dtype=np.float64)))
    sc = 128.0 ** -0.5

    def retc(n):
        out = np.zeros((128, 4, 512), np.float32)
        i = np.arange(n)
        diff = (i[None, :] - i[:, None]).astype(np.float64)
        for h in range(4):
            m = np.where(diff >= 0, np.exp(np.maximum(diff, 0) * lg[h]), 0.0) * sc
            out[0:n, 0, h * n:(h + 1) * n] = m
            out[:, 1, h * n:(h + 1) * n] = np.exp((i + 1.0) * lg[h])[None, :]
            out[0:n, 2, h * 128:(h + 1) * 128] = (np.exp((n - 1.0 - i) * lg[h]) * sc)[:, None]
            out[:, 3, h * 128:(h + 1) * 128] = np.exp(n * lg[h])
        return out

    cst = np.zeros((128, 12, 128), np.float32)
    cst[:, 0, :] = np.eye(128)
    for d in range(64):
        cst[d + 64, 1, d] = -1.0
        cst[d, 1, d + 64] = 1.0
    cst[:, 2, :] = np.eye(128)[::-1]
    cst[:, 3, :] = 1.0
    cst[:, 4, 0:64] = 1.0
    cst[:, 5, 64:128] = 1.0
    cst[0, 6, :] = 1.0
    for b in range(NSB):
        cst[1 + b, 7, b * 16:(b + 1) * 16] = 1.0

    def bucket(rel):
        nb = 16
        n = -rel
        ret = np.where(n < 0, nb, 0)
        n = np.abs(n)
        me = 8
        nf = np.maximum(n, 1).astype(np.float32)
        large = me + (np.log(nf / np.float32(me)).astype(np.float32) / np.float32(math.log(128 / me))
                      * np.float32(nb - me)).astype(np.int32)
        large = np.minimum(large, nb - 1)
        return ret + np.where(n < me, n, large)

    s = np.arange(384)
    bk = bucket(127 - s)
    oh = np.zeros((32, 384), np.float32)
    oh[bk, s] = 1.0
    j = np.arange(128)[:, None]
    ip = np.arange(256)[None, :]
    dq = ip // 64 - j // 64
    valid = ((dq >= 0) & (dq <= 2)).astype(np.float32)
    valid = np.ascontiguousarray(np.broadcast_to(valid[:, None, :], (128, 8, 256)))

    inv = (1.0 / (10000.0 ** (np.arange(64, dtype=np.float32) / np.float32(64)))).astype(np.float32)

    def cs(pos):
        ang = (pos.astype(np.float32)[:, None] * inv[None, :]).astype(np.float32)
        c = np.cos(ang.astype(np.float64)).astype(np.float32).T
        s_ = np.sin(ang.astype(np.float64)).astype(np.float32).T
        return np.ascontiguousarray(np.concatenate([c, c], 0)), np.ascontiguousarray(np.concatenate([s_, s_], 0))

    cosT, sinT = cs(np.arange(SEQ))
    c16, s16 = cs(4096 + np.arange(16))
    cosTs = np.ascontiguousarray(np.tile(c16, (1, NSB)))
    sinTs = np.ascontiguousarray(np.tile(s16, (1, NSB)))
    return dict(cst128=cst, onehot=oh, valid=valid, retc128=retc(128), retc16=retc(16), cosT=cosT, sinT=sinT,
                cosTs=cosTs, sinTs=sinTs)


def kernel(x_prompt, x_sample, c_prompt, c_sample, cache_win_k, cache_win_v, state_ret, g_mix, g_mlp, w_ada, b_ada,
           w_in, w_out, att_sinks, rel_bias, w_up, w_down, g_final):
    global _PROG
    f = lambda a: np.ascontiguousarray(np.asarray(a, dtype=np.float32))
    x_prompt, x_sample, c_prompt, c_sample = f(x_prompt), f(x_sample), f(c_prompt), f(c_sample)
    cache_win_k, cache_win_v, state_ret = f(cache_win_k), f(cache_win_v), f(state_ret)
    g_mix, g_mlp, w_ada, b_ada, w_in, w_out = f(g_mix), f(g_mlp), f(w_ada), f(b_ada), f(w_in), f(w_out)
    att_sinks, rel_bias, w_up, w_down, g_final = f(att_sinks), f(rel_bias), f(w_up), f(w_down), f(g_final)
    SEQ = x_prompt.shape[1]
    if _PROG is None or _PROG[0] != (SEQ, _STOP):
        _PROG = ((SEQ, _STOP), build_program(SEQ, _STOP))
    nc = _PROG[1]
    cs = _consts(SEQ)

    def fm(v, nchunk):
        return np.ascontiguousarray(v.reshape(nchunk, 128).T)

    gmix5 = np.stack([np.repeat(fm(g_mix[l], 8)[:, :, None], 5, 2) for l in range(NL)])
    gmlp5 = np.stack([np.repeat(fm(g_mlp[l], 8)[:, :, None], 5, 2) for l in range(NL)])
    bada5 = np.stack([np.repeat(fm(b_ada[l], 48)[:, :, None], 5, 2) for l in range(NL)])
    bada_row = np.stack([np.broadcast_to(np.stack([b_ada[l, 2048:3072], b_ada[l, 5120:6144]])[None], (5, 2, D))
                         for l in range(NL)])
    gfin_bc = np.ascontiguousarray(np.broadcast_to(g_final[None, :], (128, D)))
    sk = np.zeros((NL, 128, 4), np.float32)
    for m in range(4):
        sk[:, 0:64, m] = att_sinks[:, 2 * m][:, None]
        sk[:, 64:128, m] = att_sinks[:, 2 * m + 1][:, None]
    perm = [2 * (sl % 4) + sl // 4 for sl in range(8)]
    rel_bias_perm = np.ascontiguousarray(rel_bias[:, perm])
    zero_prompt = np.zeros_like(x_prompt[0])
    bada5_z = np.ascontiguousarray(bada5).copy()
    bada5_z[:, :, :, 0] = 0.0
    bada_row_z = np.ascontiguousarray(bada_row).copy()
    bada_row_z[:, 0] = 0.0
    in_maps = []
    for c in range(8):
        b = c % 2
        sb = slice(4 * c, 4 * c + 4)
        crow = np.concatenate([c_prompt[b:b + 1] if c < 2 else np.zeros_like(c_prompt[0:1]), c_sample[sb]], 0)
        cT = np.ascontiguousarray(crow.reshape(5, 8, 128).transpose(2, 1, 0))
        m = dict(
            xp=(x_prompt[b] if c < 2 else zero_prompt), xs=np.ascontiguousarray(x_sample[sb].reshape(NS, D)), cT=cT,
            cache_k=np.ascontiguousarray(cache_win_k[:, sb].reshape(NL, NSB, 128, 128)),
            cache_v=np.ascontiguousarray(cache_win_v[:, sb].reshape(NL, NSB, 128, 128)),
            state=np.ascontiguousarray(state_ret[:, sb].transpose(0, 1, 3, 2, 4)),
            gmix5=np.ascontiguousarray(gmix5), gmlp5=np.ascontiguousarray(gmlp5), gfin_bc=gfin_bc,
            w_ada=w_ada, bada5=(np.ascontiguousarray(bada5) if c < 2 else bada5_z),
            bada_row=(np.ascontiguousarray(bada_row) if c < 2 else bada_row_z),
            w_in=w_in, w_out=w_out, w_up=w_up, w_down=w_down, sinks=sk, rel_bias=rel_bias_perm,
            w_in_p=w_in, w_out_p=w_out, w_up_p=w_up, w_down_p=w_down,
        )
        m.update(cs)
        in_maps.append(m)
    res = run_bass_kernel_spmd(nc, in_maps, core_ids=list(range(8)))
    R = res.results
    y_prompt = np.stack([R[0]["y_p"], R[1]["y_p"]]).astype(np.float32)
    y_sample = np.concatenate([R[c]["y_s"].reshape(NSB, 16, D) for c in range(8)], 0).astype(np.float32)
    wkp = np.stack([np.stack([R[b]["wk_p"][l].reshape(128, 2, 64) for b in range(2)]) for l in range(NL)])
    wvp = np.stack([np.stack([R[b]["wv_p"][l].reshape(128, 2, 64) for b in range(2)]) for l in range(NL)])
    stp = np.stack([np.stack([R[b]["st_p"][l].transpose(1, 0, 2) for b in range(2)]) for l in range(NL)])
    wks = np.concatenate([R[c]["wk_s"].reshape(NL, NSB, 128, 2, 64) for c in range(8)], 1)
    wvs = np.concatenate([R[c]["wv_s"].reshape(NL, NSB, 128, 2, 64) for c in range(8)], 1)
    sts = np.concatenate([R[c]["st_s"].transpose(0, 1, 3, 2, 4) for c in range(8)], 1)
    return (y_prompt, y_sample, wkp.astype(np.float32), wvp.astype(np.float32), stp.astype(np.float32),
            wks.astype(np.float32), wvs.astype(np.float32), sts.astype(np.float32))
```
